# Optimizing a Trainium2 kernel written in Bass

```python
import math
import jax
import jax.numpy as jnp
from jax import lax
import numpy as np

D_MODEL = 1024
BATCH = 16
SEQ = 2048
DEPTH = 1

N_META = 16
GDN_HEADS = 4
GDN_HEAD_DIM = 128
GDN_WIDTH = GDN_HEADS * GDN_HEAD_DIM
GDN_CONV = 4
CHUNK = 64
SC_WIDTH = D_MODEL - GDN_WIDTH
SC_GROUPS = 8
SC_CONV = 3
MIX_WIDTH = GDN_WIDTH + SC_WIDTH
D_FF = -(-8 * D_MODEL // (3 * 256)) * 256
IN_SPLITS = (GDN_WIDTH, GDN_WIDTH, GDN_WIDTH, GDN_WIDTH, GDN_HEADS, GDN_HEADS, SC_WIDTH, SC_WIDTH, SC_WIDTH)
IN_WIDTH = sum(IN_SPLITS)
EPS = 1e-6

kernel_name = 'hymba_gdn_shortconv_block'


def rms_norm(x, w):
    xf = x.astype(jnp.float32)
    y = xf * lax.rsqrt(jnp.mean(xf * xf, axis=-1, keepdims=True) + EPS)
    return (y * w.astype(jnp.float32)).astype(x.dtype)


def l2_normalize(x):
    xf = x.astype(jnp.float32)
    return xf * lax.rsqrt(jnp.sum(xf * xf, axis=-1, keepdims=True) + EPS)


def causal_depthwise_conv(x, w):
    k_width = w.shape[0]
    seq_len = x.shape[1]
    xp = jnp.pad(x, ((0, 0), (k_width - 1, 0), (0, 0)))
    return sum(xp[:, i:i + seq_len] * w[i].astype(x.dtype) for i in range(k_width))


def chunked_gated_delta_rule(q, k, v, g, beta):
    b, seq_len, n_heads, dk = q.shape
    dv = v.shape[-1]
    pad = (-seq_len) % CHUNK
    f32 = jnp.float32

    def to_chunks(t):
        t = jnp.pad(t.astype(f32), ((0, 0), (pad, 0)) + ((0, 0),) * (t.ndim - 2))
        n = t.shape[1] // CHUNK
        t = t.reshape((b, n, CHUNK) + t.shape[2:])
        return jnp.moveaxis(t, 3, 1)

    qc, kc, vc, g_raw, bc = (to_chunks(t) for t in (q, k, v, g, beta))
    gc = jnp.cumsum(g_raw, axis=-1)
    idx = jnp.arange(CHUNK)
    incl = idx[:, None] >= idx[None, :]
    strict = idx[:, None] > idx[None, :]
    diff = gc[..., :, None] - gc[..., None, :]
    decay = jnp.where(incl, jnp.exp(jnp.where(incl, diff, 0.0)), 0.0)
    kb = kc * bc[..., None]
    a = jnp.where(strict, jnp.einsum('bhnid,bhnjd->bhnij', kb, kc) * decay, 0.0)
    eye = jnp.eye(CHUNK, dtype=f32)
    t_inv = lax.linalg.triangular_solve(a + eye, jnp.broadcast_to(eye, a.shape), left_side=True, lower=True)
    u = jnp.einsum('bhnij,bhnje->bhnie', t_inv, vc * bc[..., None])
    w = jnp.einsum('bhnij,bhnjd->bhnid', t_inv, kb * jnp.exp(gc)[..., None])
    qk = jnp.where(incl, jnp.einsum('bhnid,bhnjd->bhnij', qc, kc) * decay, 0.0)
    q_dec = qc * jnp.exp(gc)[..., None]
    k_dec = kc * jnp.exp(gc[..., -1:] - gc)[..., None]
    g_last = jnp.exp(gc[..., -1])

    def step(state, xs):
        q_i, k_i, u_i, w_i, qk_i, gl_i = xs
        v_new = u_i - jnp.einsum('bhcd,bhde->bhce', w_i, state)
        o_i = jnp.einsum('bhcd,bhde->bhce', q_i, state) + jnp.einsum('bhij,bhje->bhie', qk_i, v_new)
        state = state * gl_i[..., None, None] + jnp.einsum('bhcd,bhce->bhde', k_i, v_new)
        return state, o_i

    xs = tuple(jnp.moveaxis(t, 2, 0) for t in (q_dec, k_dec, u, w, qk, g_last))
    state0 = jnp.zeros((b, n_heads, dk, dv), f32)
    _, o = lax.scan(step, state0, xs)
    o = jnp.moveaxis(o, 0, 2).reshape(b, n_heads, -1, dv)
    return jnp.transpose(o, (0, 2, 1, 3))[:, pad:]


def token_mixer(u, w_in, conv_qkv, a_log, dt_bias, gdn_norm, conv_sc, w_out):
    b, seq_len, _ = u.shape
    f32 = jnp.float32
    proj = u @ w_in
    cuts = [int(c) for c in np.cumsum(IN_SPLITS)[:-1]]
    q, k, v, z, b_logit, a_logit, sc_x, sc_b, sc_c = jnp.split(proj, cuts, axis=-1)

    qkv = jax.nn.silu(causal_depthwise_conv(jnp.concatenate([q, k, v], axis=-1), conv_qkv))
    q, k, v = (t.reshape(b, seq_len, GDN_HEADS, GDN_HEAD_DIM) for t in jnp.split(qkv, 3, axis=-1))
    q = l2_normalize(q) * (GDN_HEAD_DIM ** -0.5)
    k = l2_normalize(k)
    beta = jax.nn.sigmoid(b_logit.astype(f32))
    g = -jnp.exp(a_log.astype(f32)) * jax.nn.softplus(a_logit.astype(f32) + dt_bias.astype(f32))
    o = chunked_gated_delta_rule(q, k, v, g, beta)
    gate = jax.nn.silu(z.astype(f32)).reshape(b, seq_len, GDN_HEADS, GDN_HEAD_DIM)
    o = (rms_norm(o, gdn_norm) * gate).astype(u.dtype).reshape(b, seq_len, GDN_WIDTH)

    y_sc = sc_b * causal_depthwise_conv(sc_c * sc_x, conv_sc)

    return jnp.concatenate([o, y_sc], axis=-1) @ w_out


def swiglu(u, w_gate, w_up, w_down):
    return (jax.nn.silu(u @ w_gate) * (u @ w_up)) @ w_down


def setup_inputs(seed: int = 0) -> dict:
    key = jax.random.key(seed)
    ks = jax.random.split(key, 17)
    f32 = jnp.float32

    def nrm(k, shape, scale):
        return jax.random.normal(k, shape, f32) * scale

    def gain(k, width):
        return 1.0 + 0.02 * jax.random.normal(k, (DEPTH, width), f32)

    dt = jnp.exp(jax.random.uniform(ks[9], (DEPTH, GDN_HEADS), f32, math.log(1e-3), math.log(1e-1)))
    return {
        'x': nrm(ks[0], (BATCH, SEQ, D_MODEL), 1.0),
        'meta_tokens': nrm(ks[1], (N_META, D_MODEL), 1.0),
        'mix_pre_norm': gain(ks[2], D_MODEL),
        'mix_post_norm': gain(ks[3], D_MODEL),
        'ffn_pre_norm': gain(ks[4], D_MODEL),
        'ffn_post_norm': gain(ks[5], D_MODEL),
        'w_in': nrm(ks[6], (DEPTH, D_MODEL, IN_WIDTH), D_MODEL ** -0.5),
        'conv_qkv': nrm(ks[7], (DEPTH, GDN_CONV, 3 * GDN_WIDTH), GDN_CONV ** -0.5),
        'a_log': jnp.log(jax.random.uniform(ks[8], (DEPTH, GDN_HEADS), f32, 1.0, 16.0)),
        'dt_bias': dt + jnp.log(-jnp.expm1(-dt)),
        'gdn_norm': gain(ks[10], GDN_HEAD_DIM),
        'conv_sc': nrm(ks[11], (DEPTH, SC_CONV, SC_WIDTH), SC_CONV ** -0.5),
        'w_out': nrm(ks[12], (DEPTH, MIX_WIDTH, D_MODEL), MIX_WIDTH ** -0.5),
        'w_gate': nrm(ks[13], (DEPTH, D_MODEL, D_FF), D_MODEL ** -0.5),
        'w_up': nrm(ks[14], (DEPTH, D_MODEL, D_FF), D_MODEL ** -0.5),
        'w_down': nrm(ks[15], (DEPTH, D_FF, D_MODEL), D_FF ** -0.5),
    }


def reference(x, meta_tokens, mix_pre_norm, mix_post_norm, ffn_pre_norm, ffn_post_norm, w_in, conv_qkv,
              a_log, dt_bias, gdn_norm, conv_sc, w_out, w_gate, w_up, w_down):
    b = x.shape[0]
    meta = jnp.broadcast_to(meta_tokens.astype(x.dtype)[None], (b, N_META, D_MODEL))
    h = jnp.concatenate([meta, x], axis=1)
    for l in range(DEPTH):
        mix = token_mixer(rms_norm(h, mix_pre_norm[l]), w_in[l], conv_qkv[l], a_log[l], dt_bias[l],
                          gdn_norm[l], conv_sc[l], w_out[l])
        h = h + rms_norm(mix, mix_post_norm[l])
        ffn = swiglu(rms_norm(h, ffn_pre_norm[l]), w_gate[l], w_up[l], w_down[l])
        h = h + rms_norm(ffn, ffn_post_norm[l])
    return h[:, N_META:]
```

```python
import numpy as np
import concourse.bass as bass
import concourse.mybir as mybir
from concourse.bass_utils import run_bass_kernel_spmd

F32 = mybir.dt.float32
BF16 = mybir.dt.bfloat16
ALU = mybir.AluOpType
AF = mybir.ActivationFunctionType

P = 128
D = 1024
KC = 8
NMETA = 16
H = 4
DH = 128
CH = 64
DFF = 2816
FC = 22
INW = 3592
QO, KO, VO, ZO, LGO, SXO, SBO, SCO = 0, 512, 1024, 1536, 2048, 2056, 2568, 3080
EPS = 1e-6
NEGM = -160.0
N_CORES = 8


class _Eng:
    def __init__(self, name, sem, is_pe=False):
        self.name = name
        self.sem = sem
        self.count = 0
        self.known = {}
        self.is_pe = is_pe
        self.prog = []


class Sched:
    def __init__(self, nc):
        self.nc = nc
        self._cms = []
        self.pe = _Eng("pe", self._sem("s_pe"), is_pe=True)
        self.act = _Eng("act", self._sem("s_act"))
        self.dve = _Eng("dve", self._sem("s_dve"))
        self.pool = _Eng("pool", self._sem("s_pool"))
        self.sp = _Eng("sp", self._sem("s_sp"))
        self.engs = [self.pe, self.act, self.dve, self.pool, self.sp]
        self.dsems = {}
        self.res = {}
        self.live = {}
        self.nops = 0

    def _sem(self, name):
        cm = self.nc.semaphore(name)
        s = cm.__enter__()
        self._cms.append(cm)
        return s

    def dma_sem(self, name):
        if name not in self.dsems:
            self.dsems[name] = _Eng("d_" + name, self._sem("d_" + name))
        return self.dsems[name]

    def _deps(self, reads, writes):
        need = {}

        def add(ec):
            if ec is None:
                return
            e, c = ec
            if need.get(e, 0) < c:
                need[e] = c

        for r in reads:
            st = self.res.get(r)
            if st:
                add(st[0])
        for r in writes:
            st = self.res.get(r)
            if st:
                add(st[0])
                for rd in st[1]:
                    add(rd)
        return need

    def _waits(self, eng, need):
        out = []
        for e, c in need.items():
            if e is eng and eng.is_pe:
                continue
            if eng.known.get(e, 0) >= c:
                continue
            out.append((e.sem, c))
            eng.known[e] = c
        return out

    def _commit(self, who, reads, writes):
        tag = (who, who.count)
        for r in reads:
            st = self.res.setdefault(r, [None, []])
            st[1] = [t for t in st[1] if t[0] is not who] + [tag]
        for r in writes:
            self.res[r] = [tag, []]

    def _norm(self, keys):
        out = []
        for k in keys:
            if isinstance(k, tuple) and len(k) == 3 and k[0] == "B":
                assert self.live.get(k[:2]) == k[2], "PSUM bank %s re-taken while still in use" % (k[:2],)
                k = k[:2]
            elif isinstance(k, tuple) and len(k) == 4 and k[0] == "R":
                assert self.live.get(k[:3]) == k[3], "ring buffer %s re-taken while still in use (gen %s vs %s)" % (k[:3], k[3], self.live.get(k[:3]))
                k = k[:3]
            out.append(k)
        return out

    def op(self, eng, meth, kw, reads=(), writes=()):
        reads, writes = self._norm(reads), self._norm(writes)
        pb = [r for r in reads if isinstance(r, tuple) and r[0] == "B"]
        if pb:
            writes = list(writes) + pb
        waits = self._waits(eng, self._deps(reads, writes))
        eng.count += 1
        eng.prog.append((waits, meth, kw, (eng.sem, 1), False))
        self._commit(eng, reads, writes)
        self.nops += 1

    def dma(self, q, dsem, out, in_, reads=(), writes=(), noncontig=False, defer=False):
        reads, writes = self._norm(reads), self._norm(writes)
        waits = self._waits(q, self._deps(reads, writes))
        dsem.count += 16
        q.prog.append((waits, "dma_start", dict(out=out, in_=in_), (dsem.sem, 16), noncontig))
        if defer:
            dsem.__dict__.setdefault("deferred", []).append((list(reads), list(writes)))
        else:
            self._commit(dsem, reads, writes)
        self.nops += 1

    def flush(self, dsem):
        for reads, writes in dsem.__dict__.get("deferred", []):
            self._commit(dsem, reads, writes)
        dsem.__dict__["deferred"] = []

    def alias(self, old_keys, new_keys):
        acc = {}
        for k in old_keys:
            st = self.res.get(k)
            if not st:
                continue
            for t in ([st[0]] if st[0] else []) + list(st[1]):
                if acc.get(t[0], 0) < t[1]:
                    acc[t[0]] = t[1]
        tags = [(e, c) for e, c in acc.items()]
        for k in new_keys:
            self.res[k] = [None, list(tags)]

    def final_wait(self, eng, dsems):
        for d in dsems:
            if d.count:
                eng.prog.append(([(d.sem, d.count)], None, None, None, False))

    def emit(self):
        nc = self.nc

        def replay(eng):
            def run(h):
                for waits, meth, kw, inc, noncontig in eng.prog:
                    for sem, c in waits:
                        h.wait_ge(sem, c)
                    if meth is None:
                        continue
                    if noncontig:
                        with nc.allow_non_contiguous_dma(reason="small strided parameter load"):
                            ins = getattr(h, meth)(**kw)
                    else:
                        ins = getattr(h, meth)(**kw)
                    ins.then_inc(inc[0], inc[1])
            return run

        with nc.Block() as block:
            block.sync(replay(self.sp))
            block.gpsimd(replay(self.pool))
            block.scalar(replay(self.act))
            block.vector(replay(self.dve))
            block.tensor(replay(self.pe))


class _Stop(Exception):
    pass


class Cfg:
    def __init__(self, nseq=2, seq=2048, T=512):
        self.nseq, self.seq, self.T = nseq, seq, T
        self.ntile = seq // T
        self.cols = NMETA + T
        self.nsub = T // P
        self.nch = T // CH


def bc(ap2, n):
    p, m = ap2.shape
    return ap2.unsqueeze(2).to_broadcast([p, m, n])


class Builder:
    def __init__(self, cfg, dbg=()):
        self.cfg = cfg
        self.dbg = set(dbg)
        self.dbg_out = {}
        self.nc = bass.Bass("TRN2", target_bir_lowering=False)
        self._keep = []
        self.rr_i = 0
        self.bank_gen = 0
        self.chain_pool = {"b": [6], "i": 0}
        import os
        self.stop = os.environ.get("KSTOP", "")
        self.FN = int(os.environ.get("KFN", "512"))
        self.QSTEP = int(os.environ.get("KQSTEP", "2"))
        self.POW = int(os.environ.get("KPOW", "0"))
        with self.nc.cleanup_on_exit():
            self.S = Sched(self.nc)
            self.build()
            self.nc.all_engine_barrier()

    def sb(self, name, shape, dt=F32):
        cm = self.nc.sbuf_tensor(name, list(shape), dt)
        t = cm.__enter__()
        self._keep.append(cm)
        return t

    def din(self, name, shape):
        return self.nc.dram_tensor(name, list(shape), F32, kind="ExternalInput").ap()

    def bank(self, pool=None):
        if pool is None:
            i = self.rr_i % 6
            self.rr_i += 1
        else:
            i = pool["b"][pool["i"] % len(pool["b"])]
            pool["i"] += 1
        self.bank_gen += 1
        self.S.live[("B", i)] = self.bank_gen
        return self.banks[i][:], ("B", i, self.bank_gen)

    def ring(self, name, n, shape, dt):
        return {"t": [self.sb("%s%d" % (name, i), shape, dt) for i in range(n)], "i": 0, "name": name}

    def take(self, ring):
        if ring.get("manual"):
            assert ring["free"], "ring %s exhausted" % ring["name"]
            i = ring["free"].pop(0)
        else:
            i = ring["i"] % len(ring["t"])
        ring["i"] += 1
        gen = ring["i"]
        self.S.live[("R", ring["name"], i)] = gen
        return ring["t"][i], ("R", ring["name"], i, gen)

    def rel(self, ring, *keys):
        for k in keys:
            assert k[1] == ring["name"] and k[2] not in ring["free"]
            ring["free"].append(k[2])

    def dump(self, name, ap, key, shape, dt=F32):
        if name not in self.dbg:
            return
        nm = "dbg_%s_%d" % (name, len([k for k in self.dbg_out if k.startswith("dbg_" + name)]))
        d = self.nc.dram_tensor(nm, list(shape), dt, kind="ExternalOutput").ap()
        self.dbg_out[nm] = (list(shape), dt)
        self.S.dma(self.S.sp, self.d_dbg, d, ap, reads=[key])

    def build(self):
        cfg, nc, S = self.cfg, self.nc, self.S
        T, COLS, NSUB, NCH = cfg.T, cfg.cols, cfg.nsub, cfg.nch
        NROW = cfg.nseq * cfg.seq
        self.x_d = self.din("x", [NROW, D])
        self.meta_d = self.din("meta", [NMETA, D])
        n_pre, n_post = self.din("n_pre", [D]), self.din("n_post", [D])
        n_fpre, n_fpost = self.din("n_fpre", [D]), self.din("n_fpost", [D])
        self.w_in = self.din("w_in", [D, INW])
        cqkv = self.din("cqkv", [4, 3 * 512])
        alog_d, dtb_d = self.din("a_log", [H]), self.din("dt_bias", [H])
        gdn_d = self.din("gdn", [DH])
        csc = self.din("csc", [3, 512])
        self.w_out = self.din("w_out", [D, D])
        self.w_gate, self.w_up = self.din("w_gate", [D, DFF]), self.din("w_up", [D, DFF])
        self.w_down = self.din("w_down", [DFF, D])
        self.out_d = nc.dram_tensor("out", [NROW, D], F32, kind="ExternalOutput").ap()
        self.d_dbg = S.dma_sem("dbg")
        self.d_const = S.dma_sem("const")

        self.banks = []
        for i in range(8):
            cm = nc.psum_tensor("bank%d" % i, [P, 512], F32)
            self.banks.append(cm.__enter__())
            self._keep.append(cm)

        sb = self.sb
        self.identf = sb("identf", [P, P])
        self.identb = sb("identb", [P, P], BF16)
        self.onesb = sb("onesb", [P, P], BF16)
        self.onesf = sb("onesf", [CH, P])
        self.U = sb("U", [CH, CH])
        self.SL8 = sb("SL8", [CH, 8, CH])
        self.I8f = sb("I8f", [CH, 8, CH])
        self.I8b = sb("I8b", [CH, 8, CH], BF16)
        self.NEG = [sb("NEG%d" % i, [CH, 8, CH]) for i in range(3)]
        tmp64 = sb("tmp64", [CH, CH])
        pl, dv = S.pool, S.dve
        S.op(pl, "memset", dict(ap=self.identf[:], constant=0.0), writes=["identf"])
        S.op(pl, "affine_select", dict(out=self.identf[:], in_=self.identf[:], pattern=[[-1, P]], compare_op=ALU.not_equal,
                                      fill=1.0, base=0, channel_multiplier=1), reads=["identf"], writes=["identf"])
        S.op(dv, "tensor_copy", dict(out=self.identb[:], in_=self.identf[:]), reads=["identf"], writes=["identb"])
        S.op(pl, "memset", dict(ap=self.onesb[:], constant=1.0), writes=["onesb"])
        S.op(pl, "memset", dict(ap=self.onesf[:], constant=1.0), writes=["onesf"])
        S.op(pl, "memset", dict(ap=self.U[:], constant=1.0), writes=["U"])
        S.op(pl, "affine_select", dict(out=self.U[:], in_=self.U[:], pattern=[[1, CH]], compare_op=ALU.is_ge, fill=0.0,
                                      base=0, channel_multiplier=-1), reads=["U"], writes=["U"])
        S.op(pl, "memset", dict(ap=tmp64[:], constant=1.0), writes=["tmp64"])
        S.op(pl, "affine_select", dict(out=tmp64[:], in_=tmp64[:], pattern=[[-1, CH]], compare_op=ALU.is_gt, fill=0.0,
                                      base=0, channel_multiplier=1), reads=["tmp64"], writes=["tmp64"])
        S.op(dv, "tensor_copy", dict(out=self.SL8[:], in_=tmp64[:].unsqueeze(1).to_broadcast([CH, 8, CH])),
             reads=["tmp64"], writes=["SL8"])
        S.op(dv, "tensor_copy", dict(out=self.I8f[:], in_=self.identf[0:CH, 0:CH].unsqueeze(1).to_broadcast([CH, 8, CH])),
             reads=["identf"], writes=["I8f"])
        S.op(dv, "tensor_copy", dict(out=self.I8b[:], in_=self.I8f[:]), reads=["I8f"], writes=["I8b"])
        specs = [([[-1, CH]], 1, ALU.is_gt), ([[1, CH]], -1, ALU.is_gt), ([[1, CH]], -1, ALU.is_ge)]
        for i, (pat, cm_, cop) in enumerate(specs):
            S.op(pl, "memset", dict(ap=tmp64[:], constant=0.0), reads=["tmp64"], writes=["tmp64"])
            S.op(pl, "affine_select", dict(out=tmp64[:], in_=tmp64[:], pattern=pat, compare_op=cop, fill=NEGM,
                                          base=0, channel_multiplier=cm_), reads=["tmp64"], writes=["tmp64"])
            S.op(dv, "tensor_copy", dict(out=self.NEG[i][:], in_=tmp64[:].unsqueeze(1).to_broadcast([CH, 8, CH])),
                 reads=["tmp64"], writes=["NEG%d" % i])

        self.NEGd = [sb("NEGd%d" % i, [NMETA, H * NMETA]) for i in range(3)]
        for i in range(3):
            S.op(dv, "tensor_copy", dict(out=self.NEGd[i][:].rearrange("p (b c) -> p b c", c=NMETA), in_=self.NEG[i][0:NMETA, 0:H, 0:NMETA]),
                 reads=["NEG%d" % i], writes=["NEGd%d" % i])
        self.wpre_col = sb("wpre_col", [P, KC])
        self.wf_col = sb("wf_col", [P, KC])
        self.wpost_b = sb("wpost_b", [P, D])
        self.wfpost_b = sb("wfpost_b", [P, D])
        self.gdnw = sb("gdnw", [P, 1])
        self.cwq = sb("cwq", [P, 12, 4])
        self.cws = sb("cws", [P, 4, 3])
        self.dtb = sb("dtb", [CH, H])
        self.negA = sb("negA", [CH, H])
        self.wlog = sb("wlog", [P, KC, 8], BF16)
        sp = S.sp
        dc = self.d_const
        S.dma(sp, dc, self.wpre_col[:], n_pre.rearrange("(k p) -> p k", p=P), writes=["wpre_col"], noncontig=True, defer=True)
        S.dma(sp, dc, self.wf_col[:], n_fpre.rearrange("(k p) -> p k", p=P), writes=["wf_col"], noncontig=True, defer=True)
        S.dma(sp, dc, self.wpost_b[:], n_post.partition_broadcast(P), writes=["wpost_b"], defer=True)
        S.dma(sp, dc, self.wfpost_b[:], n_fpost.partition_broadcast(P), writes=["wfpost_b"], defer=True)
        S.dma(sp, dc, self.gdnw[:], gdn_d.rearrange("(p o) -> p o", o=1), writes=["gdnw"], noncontig=True, defer=True)
        for c in range(12):
            S.dma(sp, dc, self.cwq[:, c, :], cqkv[:, c * P:(c + 1) * P].rearrange("k p -> p k"), writes=["cwq"], noncontig=True, defer=True)
        for c in range(4):
            S.dma(sp, dc, self.cws[:, c, :], csc[:, c * P:(c + 1) * P].rearrange("k p -> p k"), writes=["cws"], noncontig=True, defer=True)
        S.dma(sp, dc, self.dtb[:], dtb_d.partition_broadcast(CH), writes=["dtb"], defer=True)
        S.dma(sp, dc, self.negA[:], alog_d.partition_broadcast(CH), writes=["negA"], defer=True)
        S.flush(dc)
        S.op(S.act, "activation", dict(out=self.negA[:], in_=self.negA[:], func=AF.Exp), reads=["negA"], writes=["negA"])
        S.op(dv, "tensor_scalar", dict(out=self.negA[:], in0=self.negA[:], scalar1=-1.0, scalar2=None, op0=ALU.mult),
             reads=["negA"], writes=["negA"])
        S.dma(S.pool, S.dma_sem("wlog"), self.wlog[:], self.w_in[:, LGO:LGO + 8].rearrange("(k p) n -> p k n", p=P),
              writes=["wlog"], noncontig=True)

        self.wout = sb("wout", [P, KC, D], BF16)
        dwo = S.dma_sem("wout")
        for i in range(4):
            S.dma(S.pool, dwo, self.wout[:, 2 * i:2 * i + 2, :],
                  self.w_out[2 * i * P:(2 * i + 2) * P, :].rearrange("(k p) n -> p k n", p=P), writes=[("wout", i)], defer=True)
        S.flush(dwo)

        self.NRING = 4
        self.wring = [sb("wring%d" % i, [P, 2048], BF16) for i in range(self.NRING)]
        self.wring_sem = [S.dma_sem("wr%d" % i) for i in range(self.NRING)]
        self.pieces = self.make_pieces()
        self.piece_issued = 0

        self.xh = [sb("xh0", [P, NSUB, D])] * 2
        self.xh_ld = [S.dma_sem("xld0")] * 2
        self.xh_st = [S.dma_sem("xst0")] * 2
        self.xm = sb("xm", [NMETA, D])
        self.uT = sb("uT", [P, KC, COLS], BF16)
        self.Sst = sb("Sst", [P, H, DH])
        self.Sbf = sb("Sbf", [P, H, DH], BF16)
        self.Smeta = sb("Smeta", [P, H, DH])
        self.ptq = sb("ptq", [P, 12, 3])
        self.mtq = sb("mtq", [P, 12, 3])
        self.pts = sb("pts", [P, 4, 2])
        self.mts = sb("mts", [P, 4, 2])

        self.xs_ring = self.ring("xs", 1, [P, D], BF16)
        self.stat = sb("stat", [P, 16])
        self.colpool = self.ring("col", 9, [P, COLS + 4], F32)
        self.sqr = self.ring("sq", 4, [P, COLS], BF16)
        self.yT = sb("yT", [P, 8, COLS], BF16)
        self.g64f = self.ring("g64f", 8, [CH, 8, CH], F32)
        self.g64f.update(manual=True, free=list(range(8)))
        self.g64b = self.ring("g64b", 13, [CH, 8, CH], BF16)
        self.g64b.update(manual=True, free=list(range(13)))
        self.kbgr = self.ring("kbg", 2, [CH, 8, DH], BF16)
        self.t128 = self.ring("t128", 4, [P, 512], F32)
        self.osq = sb("osq", [P, 512], BF16)
        self.vnew = sb("vnew", [CH, H, DH], BF16)
        NS = NCH + 1
        self.NS = NS
        self.gt = {n: sb("gt_" + n, [CH, NS, H]) for n in
                   ["t1", "t2", "lnb", "beta", "x2", "m", "r", "g", "egc", "ekd", "gcs", "bge"]}
        self.gl = sb("gl", [P, NS, H])

        HB = (NCH // 2) * H
        self.HB = HB
        sizes_mix = [("kdec", HB * DH // 2), ("vb", HB * DH // 2), ("u", HB * DH), ("wT", HB * CH // 2), ("QKT", HB * CH // 2),
                     ("qnT", H * COLS // 2), ("knT", H * COLS // 2), ("vT", H * COLS // 2), ("qdT", H * COLS // 2),
                     ("zsT", H * COLS // 2)]
        sizes_ffn = [("aT", FC * T // 2), ("dn", NSUB * D)]
        tot_mix = sum(s for _, s in sizes_mix)
        tot_ffn = sum(s for _, s in sizes_ffn)
        self.arena = sb("arena", [P, max(tot_mix, tot_ffn)])
        off = 0
        av = {}
        for n, s in sizes_mix:
            av[n] = self.arena[:, off:off + s]
            off += s
        off = 0
        for n, s in sizes_ffn:
            av[n] = self.arena[:, off:off + s]
            off += s
        self.kdec = av["kdec"][0:CH, :].bitcast(BF16).rearrange("p (b d) -> p b d", d=DH)
        self.vb = av["vb"][0:CH, :].bitcast(BF16).rearrange("p (b d) -> p b d", d=DH)
        self.u = av["u"][0:CH, :].rearrange("p (b d) -> p b d", d=DH)
        self.wT = av["wT"].bitcast(BF16).rearrange("p (b c) -> p b c", c=CH)
        self.QKT = av["QKT"][0:CH, :].bitcast(BF16).rearrange("p (b c) -> p b c", c=CH)
        self.qnT = av["qnT"].bitcast(BF16).rearrange("p (h c) -> p h c", h=H)
        self.knT = av["knT"].bitcast(BF16).rearrange("p (h c) -> p h c", h=H)
        self.vT = av["vT"].bitcast(BF16).rearrange("p (h c) -> p h c", h=H)
        self.qdT = av["qdT"].bitcast(BF16).rearrange("p (h c) -> p h c", h=H)
        self.zsT = av["zsT"].bitcast(BF16).rearrange("p (h c) -> p h c", h=H)
        self.aT = av["aT"].bitcast(BF16).rearrange("p (f t) -> p f t", f=FC)
        self.dn = av["dn"].rearrange("p (s d) -> p s d", s=NSUB)
        self.mix_keys = ["kdec", "vb", "u", "wT", "QKT", "qnT", "knT", "vT", "qdT", "zsT"]
        self.ffn_keys = ["aT", "dn"]

        ntiles = cfg.nseq * cfg.ntile
        for ti in range(ntiles):
            self.tile(ti)
        self.sbuf_left = nc.sbuf_bytes_remaining
        S.final_wait(S.sp, [self.xh_st[0], self.d_dbg])
        S.emit()

    def chk(self, tag, cond=True):
        if cond and self.stop == tag:
            raise _Stop()

    def mmw(self, out, lhsT, rhs, start, stop, reads, writes):
        n = out.shape[-1]
        for c0 in range(0, n, self.FN):
            c1 = min(n, c0 + self.FN)
            self.S.op(self.S.pe, "matmul", dict(out=out[:, c0:c1], lhsT=lhsT, rhs=rhs[:, c0:c1], start=start, stop=stop),
                      reads=reads, writes=writes)

    def make_pieces(self):
        cfg = self.cfg
        pieces = []
        for ti in range(cfg.nseq * cfg.ntile):
            offs = [QO + 256 * i for i in range(8)]
            offs += [SXO, SCO, SBO, SXO + 256, SCO + 256, SBO + 256]
            for o in offs:
                pieces.append(("in", self.w_in[:, o:o + 256].rearrange("(k p) n -> p k n", p=P), "k8"))
            for j in range(11):
                pieces.append(("gate", self.w_gate[:, 256 * j:256 * j + 256].rearrange("(k p) n -> p k n", p=P), "k8"))
                pieces.append(("up", self.w_up[:, 256 * j:256 * j + 256].rearrange("(k p) n -> p k n", p=P), "k8"))
            for oh in range(2):
                for g in range(6):
                    k0, k1 = 4 * g, min(4 * g + 4, FC)
                    pieces.append(("down", self.w_down[k0 * P:k1 * P, oh * 512:(oh + 1) * 512].rearrange("(k p) n -> p k n", p=P),
                                   ("k4", k1 - k0)))
        return pieces

    def next_piece(self, expect):
        S = self.S
        idx = self.piece_cons if hasattr(self, "piece_cons") else 0
        self.piece_cons = idx + 1
        while self.piece_issued < min(len(self.pieces), idx + self.NRING - (0 if getattr(self, 'hold_one', False) else 1)):
            j = self.piece_issued
            kind, src, vs = self.pieces[j]
            slot = j % self.NRING
            if vs == "k8":
                dst = self.wring[slot][:].rearrange("p (k n) -> p k n", k=8)
            else:
                dst = self.wring[slot][:].rearrange("p (k n) -> p k n", k=4)[:, 0:vs[1], :]
            S.dma(S.pool, self.wring_sem[slot], dst, src, writes=[("wring", slot)])
            self.piece_issued += 1
        kind, src, vs = self.pieces[idx]
        assert kind == expect, (kind, expect)
        slot = idx % self.NRING
        if vs == "k8":
            view = self.wring[slot][:].rearrange("p (k n) -> p k n", k=8)
        else:
            view = self.wring[slot][:].rearrange("p (k n) -> p k n", k=4)
        return view, ("wring", slot)

    def prenorm_T(self, xh, xkey, wcol, wkey, with_meta):
        cfg, S = self.cfg, self.S
        NSUB, T = cfg.nsub, cfg.T
        st = self.stat
        xs_l = []
        jk, jkk = self.take(self.xs_ring)
        for s4 in range(NSUB):
            S.op(S.act, "activation", dict(out=jk[:], in_=xh[:, s4, :], func=AF.Square, accum_out=st[:, s4:s4 + 1]),
                 reads=[xkey], writes=[jkk, "stat"])
        S.op(S.act, "activation", dict(out=st[:, 4:4 + NSUB], in_=st[:, 0:NSUB], func=AF.Ln, scale=1.0 / D, bias=EPS),
             reads=["stat"], writes=["stat"])
        S.op(S.act, "activation", dict(out=st[:, 8:8 + NSUB], in_=st[:, 4:4 + NSUB], func=AF.Exp, scale=-0.5),
             reads=["stat"], writes=["stat"])
        if with_meta:
            S.op(S.act, "activation", dict(out=jk[0:NMETA, :], in_=self.xm[:], func=AF.Square,
                                           accum_out=st[0:NMETA, 12:13]), reads=["xm"], writes=[jkk, "stat"])
            S.op(S.act, "activation", dict(out=st[0:NMETA, 13:14], in_=st[0:NMETA, 12:13], func=AF.Ln, scale=1.0 / D, bias=EPS),
                 reads=["stat"], writes=["stat"])
            S.op(S.act, "activation", dict(out=st[0:NMETA, 14:15], in_=st[0:NMETA, 13:14], func=AF.Exp, scale=-0.5),
                 reads=["stat"], writes=["stat"])
        xs_all = []
        for s4 in range(NSUB):
            cb, cbk = self.take(self.colpool)
            xs = cb[:].bitcast(BF16)[:, 0:D]
            S.op(S.act, "activation", dict(out=xs, in_=xh[:, s4, :], func=AF.Copy, scale=st[:, 8 + s4:9 + s4]),
                 reads=[xkey, "stat"], writes=[cbk])
            xs_all.append((xs, cbk, s4))
        for kc in range(KC):
            bk, bkey = self.bank()
            bt = bk.bitcast(BF16)
            for j, (xs, cbk, s4) in enumerate(xs_all):
                S.op(S.pe, "transpose", dict(out=bt[:, j * P:(j + 1) * P], in_=xs[:, kc * P:(kc + 1) * P],
                                             identity=self.identb[:]), reads=[cbk, "identb"], writes=[bkey])
            S.op(S.dve, "tensor_scalar", dict(out=self.uT[:, kc, NMETA:NMETA + NSUB * P], in0=bt[:, 0:NSUB * P],
                                              scalar1=wcol[:, kc:kc + 1], scalar2=None, op0=ALU.mult),
                 reads=[bkey, wkey], writes=["uT"])
        if with_meta:
            xs, xk = self.take(self.xs_ring)
            S.op(S.act, "activation", dict(out=xs[0:NMETA, :], in_=self.xm[:], func=AF.Copy, scale=st[0:NMETA, 14:15]),
                 reads=["xm", "stat"], writes=[xk])
            bk, bkey = self.bank()
            bt = bk.bitcast(BF16)
            for kc in range(KC):
                S.op(S.pe, "transpose", dict(out=bt[:, kc * NMETA:(kc + 1) * NMETA], in_=xs[0:NMETA, kc * P:(kc + 1) * P],
                                             identity=self.identb[0:NMETA, 0:NMETA]), reads=[xk, "identb"], writes=[bkey])
            for kc in range(KC):
                S.op(S.dve, "tensor_scalar", dict(out=self.uT[:, kc, 0:NMETA], in0=bt[:, kc * NMETA:(kc + 1) * NMETA],
                                                  scalar1=wcol[:, kc:kc + 1], scalar2=None, op0=ALU.mult),
                     reads=[bkey, wkey], writes=["uT"])

    def proj_fm(self, wview, wkey, oc, splits):
        S = self.S
        res = []
        for (n0, n1) in splits:
            bk, bkey = self.bank(getattr(self, "fm_pool", None))
            for kc in range(KC):
                S.op(S.pe, "matmul", dict(out=bk[:, 0:n1 - n0], lhsT=wview[:, kc, oc * P:(oc + 1) * P], rhs=self.uT[:, kc, n0:n1],
                                          start=(kc == 0), stop=(kc == KC - 1)), reads=[wkey, "uT"], writes=[bkey])
            res.append((bk, bkey, n0, n1))
        return res

    def tile(self, ti):
        try:
            self.tile_(ti)
        except _Stop:
            cfg = self.cfg
            r0 = (ti // cfg.ntile) * cfg.seq + (ti % cfg.ntile) * cfg.T
            self.finish_tile(ti, self.xh[0], ("xh", 0), r0)

    def tile_(self, ti):
        cfg, S = self.cfg, self.S
        T, COLS, NSUB, NCH = cfg.T, cfg.cols, cfg.nsub, cfg.nch
        seq_i, j = ti // cfg.ntile, ti % cfg.ntile
        first_ever = (ti == 0)
        seq_start = (j == 0)
        r0 = seq_i * cfg.seq + j * T
        xh = self.xh[ti % 2]
        xkey = ("xh", 0)
        if ti > 0:
            S.alias(self.ffn_keys, self.mix_keys)
        S.dma(S.sp, self.xh_ld[ti % 2], xh[:], self.x_d[r0:r0 + T, :].rearrange("(s p) d -> p s d", p=P), writes=[xkey])
        if first_ever:
            S.dma(S.sp, S.dma_sem("xm"), self.xm[:], self.meta_d, writes=["xm"])
        c0 = 0 if first_ever else NMETA
        splits = ([(0, NMETA)] if first_ever else []) + [(NMETA + 512 * i, NMETA + 512 * (i + 1)) for i in range(T // 512)]
        rsplits = [(NMETA + 512 * i, NMETA + 512 * (i + 1)) for i in range(T // 512)]

        self.prenorm_T(xh, xkey, self.wpre_col, "wpre_col", first_ever)
        self.dump("uT", self.uT[:], "uT", [P, KC, COLS], BF16)

        if self.stop == "A":
            return self.finish_tile(ti, xh, xkey, r0)
        chunks = [(NMETA + CH * c, CH, c) for c in range(NCH)]
        if first_ever:
            chunks = [(0, NMETA, NCH)] + chunks
        plg, plg_key = self.bank({"b": [6], "i": 0})
        for (col0, C, slot) in chunks:
            for kc in range(KC):
                S.op(S.pe, "matmul", dict(out=plg[0:C, slot * 8:slot * 8 + 8], lhsT=self.uT[:, kc, col0:col0 + C],
                                          rhs=self.wlog[:, kc, :], start=(kc == 0), stop=(kc == KC - 1)),
                     reads=["uT", "wlog"], writes=[plg_key])
        if first_ever:
            for _ in self.gating(plg, plg_key, NMETA, NCH, 1):
                pass
        gate_gens = [self.gating(plg, plg_key, CH, 0, NCH, {"b": [7], "i": 0})]
        if self.stop == "C" or "g" in self.dbg:
            for gg in gate_gens:
                for _ in gg:
                    pass
            gate_gens = []
        self.dump("g", self.gt["g"][:], "gt_g", [CH, self.NS, H])
        self.dump("beta", self.gt["beta"][:], "gt_beta", [CH, self.NS, H])

        if self.stop == "C":
            return self.finish_tile(ti, xh, xkey, r0)
        active = list(gate_gens)

        def step_all(n):
            for _ in range(n):
                for g_ in list(active):
                    try:
                        next(g_)
                    except StopIteration:
                        active.remove(g_)
        for pi in range(8):
            wv, wkey = self.next_piece("in")
            for o2 in range(2):
                oc = pi * 2 + o2
                if oc < 12:
                    accs = self.proj_fm(wv, wkey, o2, splits)
                    active.append(self.evac_qkv(oc, accs, ti, c0))
                    step_all(self.QSTEP)
                else:
                    step_all(self.QSTEP)
                    accs = self.proj_fm(wv, wkey, o2, rsplits)
                    for (bk, bkey, n0, n1) in accs:
                        S.op(S.act, "activation", dict(out=self.zsT[:, oc - 12, n0:n1], in_=bk[:, 0:n1 - n0], func=AF.Silu),
                             reads=[bkey], writes=["zsT"])
        while active:
            step_all(1)
        self.dump("qnT", self.qnT, "qnT", [P, H, COLS], BF16)
        self.dump("knT", self.knT, "knT", [P, H, COLS], BF16)
        self.dump("vT", self.vT, "vT", [P, H, COLS], BF16)
        if self.stop == "B":
            return self.finish_tile(ti, xh, xkey, r0)
        def sc_gen():
            self.fm_pool = {"b": [4, 5], "i": 0}
            for half in range(2):
                sx = {}
                cv = {}
                wv, wkey = self.next_piece("in")
                for o2 in range(2):
                    cc = half * 2 + o2
                    accs = self.proj_fm(wv, wkey, o2, splits)
                    t, tk = self.take(self.colpool)
                    for (bk, bkey, n0, n1) in accs:
                        S.op(S.act, "activation", dict(out=t[:, n0:n1], in_=bk[:, 0:n1 - n0], func=AF.Copy), reads=[bkey], writes=[tk])
                    sx[cc] = (t, tk)
                    yield
                wv, wkey = self.next_piece("in")
                for o2 in range(2):
                    cc = half * 2 + o2
                    accs = self.proj_fm(wv, wkey, o2, splits)
                    raw, rk = self.take(self.colpool)
                    t, tk = sx[cc]
                    for (bk, bkey, n0, n1) in accs:
                        S.op(S.dve, "tensor_tensor", dict(out=raw[:, 2 + n0:2 + n1], in0=bk[:, 0:n1 - n0], in1=t[:, n0:n1], op=ALU.mult),
                             reads=[bkey, tk], writes=[rk])
                    yield
                    if first_ever:
                        S.op(S.dve, "memset", dict(ap=raw[:, 0:2], constant=0.0), writes=[rk])
                        S.op(S.dve, "tensor_copy", dict(out=self.mts[:, cc, :], in_=raw[:, 2 + NMETA - 2:2 + NMETA]), reads=[rk], writes=["mts"])
                    elif seq_start:
                        S.op(S.dve, "tensor_copy", dict(out=raw[:, NMETA:NMETA + 2], in_=self.mts[:, cc, :]), reads=["mts"], writes=[rk])
                    else:
                        S.op(S.dve, "tensor_copy", dict(out=raw[:, NMETA:NMETA + 2], in_=self.pts[:, cc, :]), reads=["pts"], writes=[rk])
                    S.op(S.dve, "tensor_copy", dict(out=self.pts[:, cc, :], in_=raw[:, 2 + COLS - 2:2 + COLS]), reads=[rk], writes=["pts"])
                    cvt, ck = self.take(self.colpool)
                    S.op(S.dve, "tensor_scalar", dict(out=cvt[:, c0:COLS], in0=raw[:, 2 + c0:2 + COLS], scalar1=self.cws[:, cc, 2:3],
                                                      scalar2=None, op0=ALU.mult), reads=[rk, "cws"], writes=[ck])
                    yield
                    for tap in (1, 0):
                        sh = 2 - tap
                        S.op(S.dve, "scalar_tensor_tensor", dict(out=cvt[:, c0:COLS], in0=raw[:, 2 + c0 - sh:2 + COLS - sh],
                                                                 scalar=self.cws[:, cc, tap:tap + 1], in1=cvt[:, c0:COLS],
                                                                 op0=ALU.mult, op1=ALU.add), reads=[rk, ck, "cws"], writes=[ck])
                        yield
                    cv[cc] = (cvt, ck)
                wv, wkey = self.next_piece("in")
                for o2 in range(2):
                    cc = half * 2 + o2
                    accs = self.proj_fm(wv, wkey, o2, rsplits)
                    cvt, ck = cv[cc]
                    for (bk, bkey, n0, n1) in accs:
                        S.op(S.dve, "tensor_tensor", dict(out=self.yT[:, 4 + cc, n0:n1], in0=bk[:, 0:n1 - n0], in1=cvt[:, n0:n1], op=ALU.mult),
                             reads=[bkey, ck], writes=["yT"])
                    yield

        def e_gen(s4, ebp):
            accs = []
            for oh in range(2):
                bk, bkey = self.bank(ebp)
                for kc in range(KC):
                    S.op(S.pe, "matmul", dict(out=bk[:, :], lhsT=self.yT[:, kc, NMETA + s4 * P:NMETA + (s4 + 1) * P],
                                              rhs=self.wout[:, kc, oh * 512:(oh + 1) * 512], start=(kc == 0), stop=(kc == KC - 1)),
                         reads=["yT", ("wout", kc // 2)], writes=[bkey])
                accs.append((bk, bkey))
                yield
            self.postnorm_add(accs, xh, xkey, s4, self.wpost_b, "wpost_b")
            yield

        HCH = NCH // 2

        def chain_gen(hf):
            for cp in range(HCH // 2):
                ob, okey = self.bank({"b": [7], "i": 0})
                for i in range(2):
                    cl = 2 * cp + i
                    c = hf * HCH + cl
                    yield from self.chain_chunk(NMETA + CH * c, CH, c, cl * 4, (ob, okey, i))
                yield from self.o_norm(ob, okey, NMETA + CH * (hf * HCH + 2 * cp))

        def preps(hf):
            gl_ = []
            for g in range(HCH // 2):
                cbase = hf * HCH + 2 * g
                grp = [(NMETA + CH * (cbase + i), CH, cbase + i) for i in range(2)]
                gl_.append(self.delta_prep(grp, g * 8, {"b": [2 * g, 2 * g + 1], "i": 0}))
            return gl_

        def run(gens, until=None, hold=None):
            gens = list(gens)
            must = list(until) if until is not None else list(gens)
            parked = []
            while must:
                for g_ in list(gens):
                    if g_ in parked:
                        if hold is not None and hold in gens:
                            continue
                        parked.remove(g_)
                    try:
                        r_ = next(g_)
                    except StopIteration:
                        gens.remove(g_)
                        if g_ in must:
                            must.remove(g_)
                        continue
                    if r_ == "SYNC" and hold is not None and hold in gens:
                        parked.append(g_)
            return gens

        if self.stop == "D":
            for _ in sc_gen():
                pass
            return self.finish_tile(ti, xh, xkey, r0)
        if first_ever:
            S.op(S.dve, "memset", dict(ap=self.Sst[:], constant=0.0), writes=["Sst"])
            S.op(S.dve, "memset", dict(ap=self.Sbf[:], constant=0.0), writes=["Sbf"])
            for _ in self.delta_prep([(0, NMETA, NCH)], 0):
                pass
            for _ in self.chain_chunk(0, NMETA, NCH, 0, None):
                pass
            self.dump("Sm", self.Sst[:], "Sst", [P, H, DH])
            S.op(S.act, "activation", dict(out=self.Smeta[:], in_=self.Sst[:], func=AF.Copy), reads=["Sst"], writes=["Smeta"])
        elif seq_start:
            S.op(S.act, "activation", dict(out=self.Sst[:], in_=self.Smeta[:], func=AF.Copy), reads=["Smeta"], writes=["Sst"])
            S.op(S.act, "activation", dict(out=self.Sbf[:], in_=self.Smeta[:], func=AF.Copy), reads=["Smeta"], writes=["Sbf"])
        scg = sc_gen()
        p0 = preps(0)
        rest = run([scg] + p0, until=p0)
        c0g = chain_gen(0)
        p1 = preps(1)
        rest = run(rest + [c0g] + p1, hold=c0g)
        c1g = chain_gen(1)
        self.fm_pool = None
        egs = [e_gen(s4, {"b": [2 * (s4 % 2), 2 * (s4 % 2) + 1], "i": 0}) for s4 in range(NSUB // 2)]
        run([c1g] + egs)
        self.dump("yT", self.yT[:], "yT", [P, 8, COLS], BF16)
        self.dump("S", self.Sst[:], "Sst", [P, H, DH])
        if self.stop == "DELTA":
            return self.finish_tile(ti, xh, xkey, r0)
        run([e_gen(s4, {"b": [2 * (s4 % 2), 2 * (s4 % 2) + 1], "i": 0}) for s4 in range(NSUB // 2, NSUB)])
        self.dump("h1", xh[:], xkey, [P, NSUB, D])

        if self.stop == "E":
            return self.finish_tile(ti, xh, xkey, r0)
        S.alias(self.mix_keys, self.ffn_keys)
        self.prenorm_T(xh, xkey, self.wf_col, "wf_col", False)
        self.hold_one = True
        for jp in range(11):
            wg, wgk = self.next_piece("gate")
            sgs = []
            for o2 in range(2):
                ga = self.proj_fm(wg, wgk, o2, rsplits)
                for (gb, gk, n0, n1) in ga:
                    sg, sk = self.take(self.t128)
                    S.op(S.act, "activation", dict(out=sg[:, 0:n1 - n0], in_=gb[:, 0:n1 - n0], func=AF.Silu), reads=[gk], writes=[sk])
                    sgs.append((sg, sk))
            wu, wuk = self.next_piece("up")
            for o2 in range(2):
                fc = 2 * jp + o2
                ua = self.proj_fm(wu, wuk, o2, rsplits)
                for (ub, uk, n0, n1) in ua:
                    sg, sk = sgs[o2]
                    S.op(S.dve, "tensor_tensor", dict(out=self.aT[:, fc, n0 - NMETA:n1 - NMETA], in0=ub[:, 0:n1 - n0], in1=sg[:, 0:n1 - n0],
                                                      op=ALU.mult), reads=[uk, sk], writes=["aT"])
        self.dump("u2T", self.uT[:], "uT", [P, KC, COLS], BF16)
        self.dump("aT", self.aT, "aT", [P, FC, T], BF16)
        for oh in range(2):
            accl = [self.bank() for _ in range(NSUB)]
            for g in range(6):
                wv, wk = self.next_piece("down")
                k0, k1 = 4 * g, min(4 * g + 4, FC)
                for s4 in range(NSUB):
                    bk, bkey = accl[s4]
                    for kc in range(k0, k1):
                        S.op(S.pe, "matmul", dict(out=bk[:, :], lhsT=self.aT[:, kc, s4 * P:(s4 + 1) * P], rhs=wv[:, kc - k0, :],
                                                  start=(kc == 0), stop=(kc == FC - 1)), reads=["aT", wk], writes=[bkey])
            for s4 in range(NSUB):
                bk, bkey = accl[s4]
                S.op(S.act, "activation", dict(out=self.dn[:, s4, oh * 512:(oh + 1) * 512], in_=bk[:, :], func=AF.Copy),
                     reads=[bkey], writes=["dn"])
        self.dump("dn", self.dn, "dn", [P, NSUB, D])
        self.postnorm_all(xh, xkey)
        self.finish_tile(ti, self.dn, "dn", r0)

    def finish_tile(self, ti, xh, xkey, r0):
        S, T = self.S, self.cfg.T
        src = xh[:] if not isinstance(xh, bass.AP) else xh
        S.dma(S.sp, self.xh_st[ti % 2], self.out_d[r0:r0 + T, :].rearrange("(s p) d -> p s d", p=P), src, reads=[xkey])

    def postnorm_add(self, accs, xh, xkey, s4, wb, wbkey, dst=None):
        S = self.S
        st = self.stat
        jk, jkk = self.take(self.xs_ring)
        for oh, (src, skey) in enumerate(accs):
            S.op(S.act, "activation", dict(out=jk[:, 0:512], in_=src, func=AF.Square, accum_out=st[:, oh:oh + 1]),
                 reads=[skey], writes=[jkk, "stat"])
        S.op(S.dve, "tensor_tensor", dict(out=st[:, 2:3], in0=st[:, 0:1], in1=st[:, 1:2], op=ALU.add), reads=["stat"], writes=["stat"])
        S.op(S.act, "activation", dict(out=st[:, 3:4], in_=st[:, 2:3], func=AF.Ln, scale=1.0 / D, bias=EPS), reads=["stat"], writes=["stat"])
        S.op(S.act, "activation", dict(out=st[:, 4:5], in_=st[:, 3:4], func=AF.Exp, scale=-0.5), reads=["stat"], writes=["stat"])
        for oh, (src, skey) in enumerate(accs):
            t, tk = self.take(self.t128)
            S.op(S.dve, "scalar_tensor_tensor", dict(out=t[:], in0=src, scalar=st[:, 4:5], in1=wb[:, oh * 512:(oh + 1) * 512],
                                                     op0=ALU.mult, op1=ALU.mult), reads=[skey, "stat", wbkey], writes=[tk])
            dt_, dk_ = dst if dst is not None else (xh, xkey)
            S.op(S.dve, "tensor_tensor", dict(out=dt_[:, s4, oh * 512:(oh + 1) * 512], in0=xh[:, s4, oh * 512:(oh + 1) * 512], in1=t[:],
                                              op=ALU.add), reads=[tk, xkey], writes=[dk_])

    def postnorm_all(self, xh, xkey):
        S, NSUB = self.S, self.cfg.nsub
        st = self.stat
        jk, jkk = self.take(self.xs_ring)
        for s4 in range(NSUB):
            S.op(S.act, "activation", dict(out=jk[:], in_=self.dn[:, s4, :], func=AF.Square, accum_out=st[:, s4:s4 + 1]),
                 reads=["dn"], writes=[jkk, "stat"])
        S.op(S.act, "activation", dict(out=st[:, 4:4 + NSUB], in_=st[:, 0:NSUB], func=AF.Ln, scale=1.0 / D, bias=EPS),
             reads=["stat"], writes=["stat"])
        S.op(S.act, "activation", dict(out=st[:, 8:8 + NSUB], in_=st[:, 4:4 + NSUB], func=AF.Exp, scale=-0.5),
             reads=["stat"], writes=["stat"])
        for s4 in range(NSUB):
            for oh in range(2):
                t, tk = self.take(self.t128)
                sl = slice(oh * 512, (oh + 1) * 512)
                S.op(S.dve, "scalar_tensor_tensor", dict(out=t[:], in0=self.dn[:, s4, sl], scalar=st[:, 8 + s4:9 + s4], in1=self.wfpost_b[:, sl],
                                                         op0=ALU.mult, op1=ALU.mult), reads=["dn", "stat", "wfpost_b"], writes=[tk])
                S.op(S.dve, "tensor_tensor", dict(out=self.dn[:, s4, sl], in0=xh[:, s4, sl], in1=t[:], op=ALU.add),
                     reads=[tk, xkey], writes=["dn"])

    def evac_qkv(self, oc, accs, ti, c0):
        cfg, S = self.cfg, self.S
        COLS = cfg.cols
        first_ever = (ti == 0)
        seq_start = (ti % cfg.ntile == 0)
        raw, rk = self.take(self.colpool)
        sq, sqk = self.take(self.sqr)
        for (bk, bkey, n0, n1) in accs:
            S.op(S.act, "activation", dict(out=raw[:, 3 + n0:3 + n1], in_=bk[:, 0:n1 - n0], func=AF.Copy), reads=[bkey], writes=[rk])
        if first_ever:
            S.op(S.dve, "memset", dict(ap=raw[:, 0:3], constant=0.0), writes=[rk])
            S.op(S.dve, "tensor_copy", dict(out=self.mtq[:, oc, :], in_=raw[:, 3 + NMETA - 3:3 + NMETA]), reads=[rk], writes=["mtq"])
        elif seq_start:
            S.op(S.dve, "tensor_copy", dict(out=raw[:, NMETA:NMETA + 3], in_=self.mtq[:, oc, :]), reads=["mtq"], writes=[rk])
        else:
            S.op(S.dve, "tensor_copy", dict(out=raw[:, NMETA:NMETA + 3], in_=self.ptq[:, oc, :]), reads=["ptq"], writes=[rk])
        S.op(S.dve, "tensor_copy", dict(out=self.ptq[:, oc, :], in_=raw[:, 3 + COLS - 3:3 + COLS]), reads=[rk], writes=["ptq"])
        yield
        cv, ck = self.take(self.colpool)
        S.op(S.dve, "tensor_scalar", dict(out=cv[:, c0:COLS], in0=raw[:, 3 + c0:3 + COLS], scalar1=self.cwq[:, oc, 3:4], scalar2=None,
                                          op0=ALU.mult), reads=[rk, "cwq"], writes=[ck])
        yield
        for tap in (2, 1, 0):
            sh = 3 - tap
            S.op(S.dve, "scalar_tensor_tensor", dict(out=cv[:, c0:COLS], in0=raw[:, 3 + c0 - sh:3 + COLS - sh],
                                                     scalar=self.cwq[:, oc, tap:tap + 1], in1=cv[:, c0:COLS], op0=ALU.mult, op1=ALU.add),
                 reads=[rk, ck, "cwq"], writes=[ck])
            yield
        h = oc % 4
        if oc >= 8:
            S.op(S.act, "activation", dict(out=self.vT[:, h, c0:COLS], in_=cv[:, c0:COLS], func=AF.Silu), reads=[ck], writes=["vT"])
            return
        qs, qk = self.take(self.colpool)
        S.op(S.act, "activation", dict(out=qs[:, c0:COLS], in_=cv[:, c0:COLS], func=AF.Silu), reads=[ck], writes=[qk])
        yield
        if self.POW:
            qsc = float(np.sqrt(128.0)) if oc < 4 else 1.0
            epsq = EPS * (128.0 if oc < 4 else 1.0)
            S.op(S.act, "activation", dict(out=sq[:, c0:COLS], in_=qs[:, c0:COLS], func=AF.Square, scale=qsc), reads=[qk], writes=[sqk])
            yield
            rq, rqk = self.take(self.colpool)
            for (n0, n1) in ([(0, NMETA)] if first_ever else []) + [(NMETA + 512 * i, NMETA + 512 * (i + 1)) for i in range(cfg.T // 512)]:
                bk, bkey = self.bank()
                S.op(S.pe, "matmul", dict(out=bk[:, 0:n1 - n0], lhsT=self.onesb[:], rhs=sq[:, n0:n1], start=True, stop=True),
                     reads=["onesb", sqk], writes=[bkey])
                S.op(S.dve, "tensor_scalar", dict(out=rq[:, n0:n1], in0=bk[:, 0:n1 - n0], scalar1=epsq, scalar2=None, op0=ALU.add),
                     reads=[bkey], writes=[rqk])
            yield
            S.op(S.pool, "tensor_scalar", dict(out=rq[:, c0:COLS], in0=rq[:, c0:COLS], scalar1=-0.5, scalar2=None, op0=ALU.pow),
                 reads=[rqk], writes=[rqk])
            yield
        else:
            S.op(S.act, "activation", dict(out=sq[:, c0:COLS], in_=qs[:, c0:COLS], func=AF.Square), reads=[qk], writes=[sqk])
            yield
            rq, rqk = self.take(self.colpool)
            for (n0, n1) in ([(0, NMETA)] if first_ever else []) + [(NMETA + 512 * i, NMETA + 512 * (i + 1)) for i in range(cfg.T // 512)]:
                bk, bkey = self.bank()
                S.op(S.pe, "matmul", dict(out=bk[:, 0:n1 - n0], lhsT=self.onesb[:], rhs=sq[:, n0:n1], start=True, stop=True),
                     reads=["onesb", sqk], writes=[bkey])
                S.op(S.act, "activation", dict(out=rq[:, n0:n1], in_=bk[:, 0:n1 - n0], func=AF.Ln, bias=EPS), reads=[bkey], writes=[rqk])
            yield
            if oc < 4:
                S.op(S.dve, "tensor_scalar", dict(out=rq[:, c0:COLS], in0=rq[:, c0:COLS], scalar1=float(np.log(128.0)), scalar2=None, op0=ALU.add),
                     reads=[rqk], writes=[rqk])
            S.op(S.act, "activation", dict(out=rq[:, c0:COLS], in_=rq[:, c0:COLS], func=AF.Exp, scale=-0.5), reads=[rqk], writes=[rqk])
            yield
        dst, dkey = (self.qnT, "qnT") if oc < 4 else (self.knT, "knT")
        S.op(S.dve, "tensor_tensor", dict(out=dst[:, h, c0:COLS], in0=qs[:, c0:COLS], in1=rq[:, c0:COLS], op=ALU.mult),
             reads=[qk, rqk], writes=[dkey])

    def gating(self, plg, plg_key, C, s0, n, gbp=None):
        S = self.S
        gt = self.gt
        lg = plg[0:C, s0 * 8:(s0 + n) * 8].rearrange("p (s e) -> p s e", e=8)
        b_ap, a_ap = lg[:, :, 0:4], lg[:, :, 4:8]

        def v(name):
            return gt[name][0:C, s0:s0 + n, :]
        A, Dv = S.act, S.dve
        S.op(A, "activation", dict(out=v("t1"), in_=b_ap, func=AF.Exp, scale=-1.0), reads=[plg_key], writes=["gt_t1"])
        yield
        S.op(A, "activation", dict(out=v("t2"), in_=v("t1"), func=AF.Ln, bias=1.0), reads=["gt_t1"], writes=["gt_t2"])
        yield
        S.op(Dv, "tensor_scalar", dict(out=v("lnb"), in0=v("t2"), scalar1=-1.0, scalar2=None, op0=ALU.mult), reads=["gt_t2"], writes=["gt_lnb"])
        yield
        S.op(A, "activation", dict(out=v("beta"), in_=v("t2"), func=AF.Exp, scale=-1.0), reads=["gt_t2"], writes=["gt_beta"])
        yield
        S.op(Dv, "tensor_tensor", dict(out=v("x2"), in0=a_ap, in1=self.dtb[0:C, :].unsqueeze(1).to_broadcast([C, n, H]), op=ALU.add),
             reads=[plg_key, "dtb"], writes=["gt_x2"])
        yield
        S.op(Dv, "tensor_scalar", dict(out=v("m"), in0=v("x2"), scalar1=-1.0, scalar2=None, op0=ALU.mult), reads=["gt_x2"], writes=["gt_m"])
        yield
        S.op(Dv, "tensor_tensor", dict(out=v("m"), in0=v("m"), in1=v("x2"), op=ALU.min), reads=["gt_m", "gt_x2"], writes=["gt_m"])
        yield
        S.op(A, "activation", dict(out=v("m"), in_=v("m"), func=AF.Exp), reads=["gt_m"], writes=["gt_m"])
        yield
        S.op(A, "activation", dict(out=v("m"), in_=v("m"), func=AF.Ln, bias=1.0), reads=["gt_m"], writes=["gt_m"])
        yield
        S.op(Dv, "tensor_scalar", dict(out=v("r"), in0=v("x2"), scalar1=0.0, scalar2=None, op0=ALU.max), reads=["gt_x2"], writes=["gt_r"])
        yield
        S.op(Dv, "tensor_tensor", dict(out=v("r"), in0=v("r"), in1=v("m"), op=ALU.add), reads=["gt_r", "gt_m"], writes=["gt_r"])
        yield
        S.op(Dv, "tensor_tensor", dict(out=v("g"), in0=v("r"), in1=self.negA[0:C, :].unsqueeze(1).to_broadcast([C, n, H]), op=ALU.mult),
             reads=["gt_r", "negA"], writes=["gt_g"])
        yield
        g2 = v("g").rearrange("p s h -> p (s h)")
        bk, bkey = self.bank(gbp)
        W = n * H
        S.op(S.pe, "matmul", dict(out=bk[0:C, 0:W], lhsT=self.U[0:C, 0:C], rhs=g2, start=True, stop=True), reads=["U", "gt_g"], writes=[bkey])
        yield
        S.op(S.pe, "matmul", dict(out=bk[0:C, 64:64 + W], lhsT=self.SL8[0:C, 0, 0:C], rhs=g2, start=True, stop=True),
             reads=["SL8", "gt_g"], writes=[bkey])
        yield
        S.op(S.pe, "matmul", dict(out=bk[:, 128:128 + W], lhsT=self.onesf[0:C, :], rhs=g2, start=True, stop=True),
             reads=["onesf", "gt_g"], writes=[bkey])
        yield
        S.op(A, "activation", dict(out=v("egc").rearrange("p s h -> p (s h)"), in_=bk[0:C, 0:W], func=AF.Exp), reads=[bkey], writes=["gt_egc"])
        yield
        S.op(A, "activation", dict(out=v("gcs").rearrange("p s h -> p (s h)"), in_=bk[0:C, 0:W], func=AF.Copy), reads=[bkey], writes=["gt_gcs"])
        yield
        S.op(A, "activation", dict(out=v("ekd").rearrange("p s h -> p (s h)"), in_=bk[0:C, 64:64 + W], func=AF.Exp), reads=[bkey], writes=["gt_ekd"])
        yield
        S.op(A, "activation", dict(out=self.gl[:, s0:s0 + n, :].rearrange("p s h -> p (s h)"), in_=bk[:, 128:128 + W], func=AF.Exp),
             reads=[bkey], writes=["gl"])
        yield
        S.op(Dv, "tensor_tensor", dict(out=v("bge"), in0=v("beta"), in1=v("egc"), op=ALU.mult), reads=["gt_beta", "gt_egc"], writes=["gt_bge"])
        yield

    def delta_prep(self, grp, b0, bp=None):
        S = self.S
        C = grp[0][1]
        ncl = len(grp)
        nb = ncl * H
        s0 = grp[0][2]
        gt = self.gt
        A, Dv, PE = S.act, S.dve, S.pe
        if bp is None:
            bp = {"b": [0, 1], "i": 0}
        bank = lambda: self.bank(bp)

        def gv(name):
            return gt[name][0:C, s0:s0 + ncl, :].rearrange("p s h -> p (s h)")
        f32v = lambda t: t[0:C, 0:nb, 0:C]
        bfv = f32v
        flat = lambda ap: ap.rearrange("p b c -> p (b c)")
        dn3 = lambda t: t[0:C].rearrange("p b c -> p (b c)")[:, 0:nb * C].rearrange("p (b c) -> p b c", c=C)
        v3 = lambda bkp: bkp[0:C, 0:nb * C].rearrange("p (b c) -> p b c", c=C)
        negf = (lambda i: flat(self.NEG[i][:])) if C == CH else (lambda i: self.NEGd[i][:])
        tagc = "m" if C != CH else "r"
        dodump = not hasattr(self, "_dumped_" + tagc)
        setattr(self, "_dumped_" + tagc, True)

        gSL, gSLk = self.take(self.g64f)
        gSLd = dn3(gSL)
        S.op(Dv, "tensor_tensor", dict(out=gSLd, in0=f32v(self.SL8), in1=bc(gv("g"), C), op=ALU.mult), reads=["SL8", "gt_g"], writes=[gSLk])
        dlnb, dlnbk = self.take(self.g64f)
        dlnbd = dn3(dlnb)
        S.op(Dv, "tensor_tensor", dict(out=dlnbd, in0=f32v(self.I8f), in1=bc(gv("lnb"), C), op=ALU.mult), reads=["I8f", "gt_lnb"], writes=[dlnbk])
        bKK, kKK = bank()
        for cl, (col0, _, _) in enumerate(grp):
            for h in range(H):
                b = cl * H + h
                S.op(PE, "matmul", dict(out=bKK[0:C, b * C:(b + 1) * C], lhsT=self.knT[:, h, col0:col0 + C], rhs=self.knT[:, h, col0:col0 + C],
                                        start=True, stop=True), reads=["knT"], writes=[kKK])
        yield
        nKK, nKKk = self.take(self.g64f)
        S.op(A, "activation", dict(out=f32v(nKK), in_=v3(bKK), func=AF.Copy, scale=-1.0), reads=[kKK], writes=[nKKk])
        bA, kA = bank()
        self.mmw(bA[0:C, 0:nb * C], self.U[0:C, 0:C], flat(gSLd), True, False, ["U", gSLk], [kA])
        self.mmw(bA[0:C, 0:nb * C], self.identf[0:C, 0:C], negf(0), False, False, ["identf", "NEG0", "NEGd0"], [kA])
        for b in range(nb):
            S.op(PE, "matmul", dict(out=bA[0:C, b * C:(b + 1) * C], lhsT=dlnbd[:, b, :], rhs=self.onesf[0:C, 0:C], start=False, stop=(b == nb - 1)),
                 reads=[dlnbk, "onesf"], writes=[kA])
        yield
        eA, eAk = self.take(self.g64f)
        S.op(A, "activation", dict(out=f32v(eA), in_=v3(bA), func=AF.Exp), reads=[kA], writes=[eAk])
        if dodump:
            self.dump("eA" + tagc, eA[:], eAk, [CH, 8, CH], F32)
        Y, Yk = self.take(self.g64b)
        S.op(Dv, "tensor_tensor", dict(out=bfv(Y), in0=f32v(nKK), in1=f32v(eA), op=ALU.mult), reads=[nKKk, eAk], writes=[Yk])
        self.rel(self.g64f, eAk)
        bB, kB = bank()
        self.mmw(bB[0:C, 0:nb * C], self.identf[0:C, 0:C], negf(1), True, False, ["identf", "NEG1", "NEGd1"], [kB])
        for b in range(nb):
            S.op(PE, "matmul", dict(out=bB[0:C, b * C:(b + 1) * C], lhsT=gSLd[:, b, :], rhs=self.U[0:C, 0:C], start=False, stop=False),
                 reads=[gSLk, "U"], writes=[kB])
        self.mmw(bB[0:C, 0:nb * C], self.onesf[0:C, 0:C], flat(dlnbd), False, True, ["onesf", dlnbk], [kB])
        yield
        eB, eBk = self.take(self.g64f)
        S.op(A, "activation", dict(out=f32v(eB), in_=v3(bB), func=AF.Exp), reads=[kB], writes=[eBk])
        X, Xk = self.take(self.g64b)
        S.op(Dv, "tensor_tensor", dict(out=bfv(X), in0=f32v(nKK), in1=f32v(eB), op=ALU.mult), reads=[nKKk, eBk], writes=[Xk])
        self.rel(self.g64f, eBk, nKKk)
        Pm, Pk = self.take(self.g64b)
        S.op(Dv, "tensor_tensor", dict(out=bfv(Pm), in0=bfv(X), in1=self.I8b[0:C, 0:nb, 0:C], op=ALU.add), reads=[Xk, "I8b"], writes=[Pk])
        bQ, kQ = bank()
        self.mmw(bQ[0:C, 0:nb * C], self.identf[0:C, 0:C], negf(2), True, False, ["identf", "NEG2", "NEGd2"], [kQ])
        for b in range(nb):
            S.op(PE, "matmul", dict(out=bQ[0:C, b * C:(b + 1) * C], lhsT=gSLd[:, b, :], rhs=self.U[0:C, 0:C], start=False, stop=(b == nb - 1)),
                 reads=[gSLk, "U"], writes=[kQ])
        self.rel(self.g64f, gSLk, dlnbk)
        yield
        eQ, eQk = self.take(self.g64f)
        S.op(A, "activation", dict(out=f32v(eQ), in_=v3(bQ), func=AF.Exp), reads=[kQ], writes=[eQk])
        bKQ, kKQ = bank()
        for cl, (col0, _, _) in enumerate(grp):
            for h in range(H):
                b = cl * H + h
                S.op(PE, "matmul", dict(out=bKQ[0:C, b * C:(b + 1) * C], lhsT=self.knT[:, h, col0:col0 + C], rhs=self.qnT[:, h, col0:col0 + C],
                                        start=True, stop=True), reads=["knT", "qnT"], writes=[kKQ])
        yield
        QKTt, QKTtk = self.take(self.g64b)
        S.op(Dv, "tensor_tensor", dict(out=QKTt[0:C, 0:nb, 0:C], in0=v3(bKQ), in1=f32v(eQ), op=ALU.mult), reads=[kKQ, eQk], writes=[QKTtk])
        self.rel(self.g64f, eQk)
        nlev = 5 if C == CH else 3
        for lev in range(1, nlev + 1):
            last = (lev == nlev)
            bY, kY = bank()
            for b in range(nb):
                S.op(PE, "matmul", dict(out=bY[0:C, b * C:(b + 1) * C], lhsT=X[0:C, b, 0:C], rhs=Y[0:C, b, 0:C], start=True, stop=True),
                     reads=[Xk, Yk], writes=[kY])
            if not last:
                bX, kX = bank()
                for b in range(nb):
                    S.op(PE, "matmul", dict(out=bX[0:C, b * C:(b + 1) * C], lhsT=Y[0:C, b, 0:C], rhs=X[0:C, b, 0:C], start=True, stop=True),
                         reads=[Xk, Yk], writes=[kX])
            self.rel(self.g64b, Xk, Yk)
            yield
            IY, IYk = self.take(self.g64b)
            S.op(Dv, "tensor_tensor", dict(out=bfv(IY), in0=v3(bY), in1=f32v(self.I8f), op=ALU.add), reads=[kY, "I8f"], writes=[IYk])
            if not last:
                Yn, Ynk = self.take(self.g64b)
                Xn, Xnk = self.take(self.g64b)
                S.op(A, "activation", dict(out=bfv(Yn), in_=v3(bY), func=AF.Copy), reads=[kY], writes=[Ynk])
                S.op(A, "activation", dict(out=bfv(Xn), in_=v3(bX), func=AF.Copy), reads=[kX], writes=[Xnk])
            bP, kP = bank()
            for b in range(nb):
                S.op(PE, "matmul", dict(out=bP[0:C, b * C:(b + 1) * C], lhsT=IY[0:C, b, 0:C], rhs=Pm[0:C, b, 0:C], start=True, stop=True),
                     reads=[IYk, Pk], writes=[kP])
            self.rel(self.g64b, IYk, Pk)
            yield
            Pn, Pnk = self.take(self.g64b)
            S.op(A, "activation", dict(out=bfv(Pn), in_=v3(bP), func=AF.Copy), reads=[kP], writes=[Pnk])
            Pm, Pk = Pn, Pnk
            if not last:
                X, Xk, Y, Yk = Xn, Xnk, Yn, Ynk
        TT, TTk = Pm, Pk
        if dodump:
            self.dump("TT" + tagc, TT[:], TTk, [CH, 8, CH], BF16)
        dgc, dgck = self.take(self.g64f)
        dgcd = dn3(dgc)
        S.op(Dv, "tensor_tensor", dict(out=dgcd, in0=f32v(self.I8f), in1=bc(gv("gcs"), C), op=ALU.mult), reads=["I8f", "gt_gcs"], writes=[dgck])
        bE, kE = bank()
        self.mmw(bE[:, 0:nb * C], self.onesf[0:C, :], flat(dgcd), True, True, ["onesf", dgck], [kE])
        self.rel(self.g64f, dgck)
        yield
        Eg, Egk = self.take(self.colpool)
        S.op(A, "activation", dict(out=Eg[:, 0:nb * C], in_=bE[:, 0:nb * C], func=AF.Exp), reads=[kE], writes=[Egk])
        for cl, (colc, _, _) in enumerate(grp):
            Ev = Eg[:, cl * H * C:(cl + 1) * H * C].rearrange("p (h i) -> p h i", h=H)
            S.op(Dv, "tensor_tensor", dict(out=self.qdT[:, :, colc:colc + C], in0=self.qnT[:, :, colc:colc + C], in1=Ev, op=ALU.mult),
                 reads=["qnT", Egk], writes=["qdT"])
        yield "SYNC"
        S.op(S.pool, "tensor_copy", dict(out=self.QKT[0:C, b0:b0 + nb, 0:C], in_=QKTt[0:C, 0:nb, 0:C]), reads=[QKTtk], writes=["QKT"])
        self.rel(self.g64b, QKTtk)
        bK, kK = bank()
        bV, kV = bank()
        bKt, bVt = bK.bitcast(BF16), bV.bitcast(BF16)
        for cl, (col0, _, _) in enumerate(grp):
            for h in range(H):
                b = cl * H + h
                S.op(PE, "transpose", dict(out=bKt[0:C, b * DH:(b + 1) * DH], in_=self.knT[:, h, col0:col0 + C], identity=self.identb[:]),
                     reads=["knT", "identb"], writes=[kK])
        for cl, (col0, _, _) in enumerate(grp):
            for h in range(H):
                b = cl * H + h
                S.op(PE, "transpose", dict(out=bVt[0:C, b * DH:(b + 1) * DH], in_=self.vT[:, h, col0:col0 + C], identity=self.identb[:]),
                     reads=["vT", "identb"], writes=[kV])
        yield
        k3 = bKt[0:C, 0:nb * DH].rearrange("p (b d) -> p b d", d=DH)
        v3_ = bVt[0:C, 0:nb * DH].rearrange("p (b d) -> p b d", d=DH)
        kbg, kbgk = self.take(self.kbgr)
        S.op(Dv, "tensor_tensor", dict(out=kbg[0:C, 0:nb, :], in0=k3, in1=bc(gv("bge"), DH), op=ALU.mult), reads=[kK, "gt_bge"], writes=[kbgk])
        S.op(Dv, "tensor_tensor", dict(out=self.kdec[0:C, b0:b0 + nb, :], in0=k3, in1=bc(gv("ekd"), DH), op=ALU.mult), reads=[kK, "gt_ekd"], writes=["kdec"])
        S.op(Dv, "tensor_tensor", dict(out=self.vb[0:C, b0:b0 + nb, :], in0=v3_, in1=bc(gv("beta"), DH), op=ALU.mult), reads=[kV, "gt_beta"], writes=["vb"])
        bW, kW = bank()
        for b in range(nb):
            S.op(PE, "matmul", dict(out=bW[:, b * C:(b + 1) * C], lhsT=kbg[0:C, b, :], rhs=TT[0:C, b, 0:C], start=True, stop=True),
                 reads=[kbgk, TTk], writes=[kW])
        yield
        S.op(A, "activation", dict(out=self.wT[:, b0:b0 + nb, 0:C], in_=bW[:, 0:nb * C].rearrange("p (b c) -> p b c", c=C), func=AF.Copy),
             reads=[kW], writes=["wT"])
        for q in range(0, nb, 4):
            bU, kU = bank()
            for b in range(q, min(q + 4, nb)):
                S.op(PE, "matmul", dict(out=bU[0:C, (b - q) * DH:(b - q + 1) * DH], lhsT=TT[0:C, b, 0:C], rhs=self.vb[0:C, b0 + b, :],
                                        start=True, stop=True), reads=[TTk, "vb"], writes=[kU])
            yield
            nq = min(4, nb - q)
            S.op(A, "activation", dict(out=self.u[0:C, b0 + q:b0 + q + nq, :], in_=bU[0:C, 0:nq * DH].rearrange("p (b d) -> p b d", d=DH), func=AF.Copy),
                 reads=[kU], writes=["u"])
        self.rel(self.g64b, TTk)
        if dodump:
            self.dump("u" + tagc, self.u, "u", [CH, self.HB, DH], F32)
            self.dump("wT" + tagc, self.wT, "wT", [P, self.HB, CH], BF16)
            self.dump("QKT" + tagc, self.QKT, "QKT", [CH, self.HB, CH], BF16)
            self.dump("kdec" + tagc, self.kdec, "kdec", [CH, self.HB, DH], BF16)
            self.dump("vb" + tagc, self.vb, "vb", [CH, self.HB, DH], BF16)

    def chain_chunk(self, col0, C, slot, bb, oinfo):
        S = self.S
        A, Dv, PE = S.act, S.dve, S.pe
        bWS, kWS = self.bank(self.chain_pool)
        for h in range(H):
            S.op(PE, "matmul", dict(out=bWS[0:C, h * DH:(h + 1) * DH], lhsT=self.wT[:, bb + h, 0:C], rhs=self.Sbf[:, h, :], start=True, stop=True),
                 reads=["wT", "Sbf"], writes=[kWS])
        S.op(Dv, "tensor_tensor", dict(out=self.vnew[0:C, :, :], in0=self.u[0:C, bb:bb + H, :],
                                       in1=bWS[0:C, :].rearrange("p (h d) -> p h d", d=DH), op=ALU.subtract),
             reads=["u", kWS], writes=["vnew"])
        yield
        if oinfo is not None:
            ob, okey, i = oinfo
            for h in range(H):
                reg = ob[:, (i * H + h) * CH:(i * H + h + 1) * CH]
                S.op(PE, "matmul", dict(out=reg, lhsT=self.Sbf[:, h, :], rhs=self.qdT[:, h, col0:col0 + C], start=True, stop=False),
                     reads=["Sbf", "qdT"], writes=[okey])
                S.op(PE, "matmul", dict(out=reg, lhsT=self.vnew[0:C, h, :], rhs=self.QKT[0:C, bb + h, 0:C], start=False, stop=True),
                     reads=["vnew", "QKT"], writes=[okey])
        yield
        bSd, kSd = self.bank(self.chain_pool)
        for h in range(H):
            S.op(PE, "matmul", dict(out=bSd[:, h * DH:(h + 1) * DH], lhsT=self.kdec[0:C, bb + h, :], rhs=self.vnew[0:C, h, :], start=True, stop=True),
                 reads=["kdec", "vnew"], writes=[kSd])
        yield
        for h in range(H):
            S.op(Dv, "scalar_tensor_tensor", dict(out=self.Sst[:, h, :], in0=self.Sst[:, h, :], scalar=self.gl[:, slot, h:h + 1],
                                                  in1=bSd[:, h * DH:(h + 1) * DH], op0=ALU.mult, op1=ALU.add),
                 reads=["Sst", "gl", kSd], writes=["Sst"])
        yield
        S.op(A, "activation", dict(out=self.Sbf[:], in_=self.Sst[:], func=AF.Copy), reads=["Sst"], writes=["Sbf"])
        yield

    def o_norm(self, ob, okey, col0):
        S = self.S
        A, Dv, PE = S.act, S.dve, S.pe
        osb, osbk = self.take(self.t128)
        S.op(A, "activation", dict(out=osb[:], in_=ob[:, :], func=AF.Copy), reads=[okey], writes=[osbk])
        S.op(A, "activation", dict(out=self.osq[:], in_=ob[:, :], func=AF.Square), reads=[okey], writes=["osq"])
        yield
        bN, kN = self.bank(self.chain_pool)
        S.op(PE, "matmul", dict(out=bN[:, :], lhsT=self.onesb[:], rhs=self.osq[:], start=True, stop=True), reads=["onesb", "osq"], writes=[kN])
        yield
        rs, rsk = self.take(self.t128)
        S.op(A, "activation", dict(out=rs[:], in_=bN[:, :], func=AF.Ln, scale=1.0 / DH, bias=EPS), reads=[kN], writes=[rsk])
        S.op(A, "activation", dict(out=rs[:], in_=rs[:], func=AF.Exp, scale=-0.5), reads=[rsk], writes=[rsk])
        yield
        S.op(Dv, "tensor_tensor", dict(out=osb[:], in0=osb[:], in1=rs[:], op=ALU.mult), reads=[osbk, rsk], writes=[osbk])
        for c in range(2):
            ov = osb[:, c * H * CH:(c + 1) * H * CH].rearrange("p (h i) -> p h i", h=H)
            cc0 = col0 + c * CH
            S.op(Dv, "scalar_tensor_tensor", dict(out=self.yT[:, 0:H, cc0:cc0 + CH], in0=ov, scalar=self.gdnw[:, 0:1],
                                                  in1=self.zsT[:, :, cc0:cc0 + CH], op0=ALU.mult, op1=ALU.mult),
                 reads=[osbk, "gdnw", "zsT"], writes=["yT"])


_CACHE = {}


def _get_builder(cfg_key, dbg=()):
    key = (cfg_key, tuple(dbg))
    if key not in _CACHE:
        _CACHE[key] = Builder(Cfg(*cfg_key), dbg)
    return _CACHE[key]


def make_in_maps(cfg, n_cores, x, meta_tokens, mix_pre_norm, mix_post_norm, ffn_pre_norm, ffn_post_norm, w_in, conv_qkv,
                 a_log, dt_bias, gdn_norm, conv_sc, w_out, w_gate, w_up, w_down):
    f = lambda a: np.ascontiguousarray(np.asarray(a, dtype=np.float32))
    shared = {
        "meta": f(meta_tokens), "n_pre": f(mix_pre_norm).reshape(D), "n_post": f(mix_post_norm).reshape(D),
        "n_fpre": f(ffn_pre_norm).reshape(D), "n_fpost": f(ffn_post_norm).reshape(D),
        "w_in": f(w_in).reshape(D, INW), "cqkv": f(conv_qkv).reshape(4, 1536), "a_log": f(a_log).reshape(H),
        "dt_bias": f(dt_bias).reshape(H), "gdn": f(gdn_norm).reshape(DH), "csc": f(conv_sc).reshape(3, 512),
        "w_out": f(w_out).reshape(D, D), "w_gate": f(w_gate).reshape(D, DFF), "w_up": f(w_up).reshape(D, DFF),
        "w_down": f(w_down).reshape(DFF, D),
    }
    x = f(x)
    maps = []
    for c in range(n_cores):
        m = dict(shared)
        m["x"] = np.ascontiguousarray(x[c * cfg.nseq:(c + 1) * cfg.nseq].reshape(cfg.nseq * cfg.seq, D))
        maps.append(m)
    return maps


def kernel(x, meta_tokens, mix_pre_norm, mix_post_norm, ffn_pre_norm, ffn_post_norm, w_in, conv_qkv,
           a_log, dt_bias, gdn_norm, conv_sc, w_out, w_gate, w_up, w_down):
    x = np.asarray(x)
    B, SEQ, _ = x.shape
    nseq = B // N_CORES
    b = _get_builder((nseq, SEQ, 512))
    maps = make_in_maps(b.cfg, N_CORES, x, meta_tokens, mix_pre_norm, mix_post_norm, ffn_pre_norm, ffn_post_norm, w_in,
                        conv_qkv, a_log, dt_bias, gdn_norm, conv_sc, w_out, w_gate, w_up, w_down)
    res = run_bass_kernel_spmd(b.nc, maps, core_ids=list(range(N_CORES)))
    outs = [np.asarray(r["out"]).reshape(nseq, SEQ, D) for r in res.results]
    return np.concatenate(outs, axis=0).astype(np.float32)
```

```python
import numpy as np
import concourse.bass as bass
import concourse.mybir as mybir
from concourse.bass_utils import run_bass_kernel_spmd

F32 = mybir.dt.float32
BF16 = mybir.dt.bfloat16
ALU = mybir.AluOpType
AF = mybir.ActivationFunctionType

P = 128
D = 1024
KC = 8
NMETA = 16
H = 4
DH = 128
CH = 64
DFF = 2816
FC = 22
INW = 3592
QO, KO, VO, ZO, LGO, SXO, SBO, SCO = 0, 512, 1024, 1536, 2048, 2056, 2568, 3080
EPS = 1e-6
NEGM = -160.0
N_CORES = 8


class _Eng:
    def __init__(self, name, sem, is_pe=False):
        self.name = name
        self.sem = sem
        self.count = 0
        self.known = {}
        self.is_pe = is_pe
        self.prog = []


class Sched:
    def __init__(self, nc):
        self.nc = nc
        self._cms = []
        self.pe = _Eng("pe", self._sem("s_pe"), is_pe=True)
        self.act = _Eng("act", self._sem("s_act"))
        self.dve = _Eng("dve", self._sem("s_dve"))
        self.pool = _Eng("pool", self._sem("s_pool"))
        self.sp = _Eng("sp", self._sem("s_sp"))
        self.engs = [self.pe, self.act, self.dve, self.pool, self.sp]
        self.dsems = {}
        self.res = {}
        self.live = {}
        self.nops = 0

    def _sem(self, name):
        cm = self.nc.semaphore(name)
        s = cm.__enter__()
        self._cms.append(cm)
        return s

    def dma_sem(self, name):
        if name not in self.dsems:
            self.dsems[name] = _Eng("d_" + name, self._sem("d_" + name))
        return self.dsems[name]

    def _deps(self, reads, writes):
        need = {}

        def add(ec):
            if ec is None:
                return
            e, c = ec
            if need.get(e, 0) < c:
                need[e] = c

        for r in reads:
            st = self.res.get(r)
            if st:
                add(st[0])
        for r in writes:
            st = self.res.get(r)
            if st:
                add(st[0])
                for rd in st[1]:
                    add(rd)
        return need

    def _waits(self, eng, need):
        out = []
        for e, c in need.items():
            if e is eng and eng.is_pe:
                continue
            if eng.known.get(e, 0) >= c:
                continue
            out.append((e.sem, c))
            eng.known[e] = c
        return out

    def _commit(self, who, reads, writes):
        tag = (who, who.count)
        for r in reads:
            st = self.res.setdefault(r, [None, []])
            st[1] = [t for t in st[1] if t[0] is not who] + [tag]
        for r in writes:
            self.res[r] = [tag, []]

    def _norm(self, keys):
        out = []
        for k in keys:
            if isinstance(k, tuple) and len(k) == 3 and k[0] == "B":
                assert self.live.get(k[:2]) == k[2], "PSUM bank %s re-taken while still in use" % (k[:2],)
                k = k[:2]
            elif isinstance(k, tuple) and len(k) == 4 and k[0] == "R":
                assert self.live.get(k[:3]) == k[3], "ring buffer %s re-taken while still in use (gen %s vs %s)" % (k[:3], k[3], self.live.get(k[:3]))
                k = k[:3]
            out.append(k)
        return out

    def op(self, eng, meth, kw, reads=(), writes=()):
        reads, writes = self._norm(reads), self._norm(writes)
        pb = [r for r in reads if isinstance(r, tuple) and r[0] == "B"]
        if pb:
            writes = list(writes) + pb
        waits = self._waits(eng, self._deps(reads, writes))
        eng.count += 1
        eng.prog.append((waits, meth, kw, (eng.sem, 1), False))
        self._commit(eng, reads, writes)
        self.nops += 1

    def dma(self, q, dsem, out, in_, reads=(), writes=(), noncontig=False, defer=False):
        reads, writes = self._norm(reads), self._norm(writes)
        waits = self._waits(q, self._deps(reads, writes))
        dsem.count += 16
        q.prog.append((waits, "dma_start", dict(out=out, in_=in_), (dsem.sem, 16), noncontig))
        if defer:
            dsem.__dict__.setdefault("deferred", []).append((list(reads), list(writes)))
        else:
            self._commit(dsem, reads, writes)
        self.nops += 1

    def flush(self, dsem):
        for reads, writes in dsem.__dict__.get("deferred", []):
            self._commit(dsem, reads, writes)
        dsem.__dict__["deferred"] = []

    def alias(self, old_keys, new_keys):
        acc = {}
        for k in old_keys:
            st = self.res.get(k)
            if not st:
                continue
            for t in ([st[0]] if st[0] else []) + list(st[1]):
                if acc.get(t[0], 0) < t[1]:
                    acc[t[0]] = t[1]
        tags = [(e, c) for e, c in acc.items()]
        for k in new_keys:
            self.res[k] = [None, list(tags)]

    def final_wait(self, eng, dsems):
        for d in dsems:
            if d.count:
                eng.prog.append(([(d.sem, d.count)], None, None, None, False))

    def emit(self):
        nc = self.nc

        def replay(eng):
            def run(h):
                for waits, meth, kw, inc, noncontig in eng.prog:
                    for sem, c in waits:
                        h.wait_ge(sem, c)
                    if meth is None:
                        continue
                    if noncontig:
                        with nc.allow_non_contiguous_dma(reason="small strided parameter load"):
                            ins = getattr(h, meth)(**kw)
                    else:
                        ins = getattr(h, meth)(**kw)
                    ins.then_inc(inc[0], inc[1])
            return run

        with nc.Block() as block:
            block.sync(replay(self.sp))
            block.gpsimd(replay(self.pool))
            block.scalar(replay(self.act))
            block.vector(replay(self.dve))
            block.tensor(replay(self.pe))


class _Stop(Exception):
    pass


class Cfg:
    def __init__(self, nseq=2, seq=2048, T=512):
        self.nseq, self.seq, self.T = nseq, seq, T
        self.ntile = seq // T
        self.cols = NMETA + T
        self.nsub = T // P
        self.nch = T // CH


def bc(ap2, n):
    p, m = ap2.shape
    return ap2.unsqueeze(2).to_broadcast([p, m, n])


class Builder:
    def __init__(self, cfg, dbg=()):
        self.cfg = cfg
        self.dbg = set(dbg)
        self.dbg_out = {}
        self.nc = bass.Bass("TRN2", target_bir_lowering=False)
        self._keep = []
        self.rr_i = 0
        self.bank_gen = 0
        self.rr_n = 6
        self.chain_pool = {"b": [6], "i": 0}
        import os
        self.stop = os.environ.get("KSTOP", "")
        self.FN = int(os.environ.get("KFN", "512"))
        self.QSTEP = int(os.environ.get("KQSTEP", "2"))
        self.POW = int(os.environ.get("KPOW", "0"))
        with self.nc.cleanup_on_exit():
            self.S = Sched(self.nc)
            self.build()
            self.nc.all_engine_barrier()

    def sb(self, name, shape, dt=F32):
        cm = self.nc.sbuf_tensor(name, list(shape), dt)
        t = cm.__enter__()
        self._keep.append(cm)
        return t

    def din(self, name, shape):
        return self.nc.dram_tensor(name, list(shape), F32, kind="ExternalInput").ap()

    def bank(self, pool=None):
        if pool is None:
            i = self.rr_i % self.rr_n
            self.rr_i += 1
        else:
            i = pool["b"][pool["i"] % len(pool["b"])]
            pool["i"] += 1
        self.bank_gen += 1
        self.S.live[("B", i)] = self.bank_gen
        return self.banks[i][:], ("B", i, self.bank_gen)

    def ring(self, name, n, shape, dt):
        return {"t": [self.sb("%s%d" % (name, i), shape, dt) for i in range(n)], "i": 0, "name": name}

    def take(self, ring):
        if ring.get("manual"):
            assert ring["free"], "ring %s exhausted" % ring["name"]
            i = ring["free"].pop(0)
        else:
            i = ring["i"] % len(ring["t"])
        ring["i"] += 1
        gen = ring["i"]
        self.S.live[("R", ring["name"], i)] = gen
        return ring["t"][i], ("R", ring["name"], i, gen)

    def rel(self, ring, *keys):
        for k in keys:
            assert k[1] == ring["name"] and k[2] not in ring["free"]
            ring["free"].append(k[2])

    def dump(self, name, ap, key, shape, dt=F32):
        if name not in self.dbg:
            return
        nm = "dbg_%s_%d" % (name, len([k for k in self.dbg_out if k.startswith("dbg_" + name)]))
        d = self.nc.dram_tensor(nm, list(shape), dt, kind="ExternalOutput").ap()
        self.dbg_out[nm] = (list(shape), dt)
        self.S.dma(self.S.sp, self.d_dbg, d, ap, reads=[key])

    def build(self):
        cfg, nc, S = self.cfg, self.nc, self.S
        T, COLS, NSUB, NCH = cfg.T, cfg.cols, cfg.nsub, cfg.nch
        NROW = cfg.nseq * cfg.seq
        self.x_d = self.din("x", [NROW, D])
        self.meta_d = self.din("meta", [NMETA, D])
        n_pre, n_post = self.din("n_pre", [D]), self.din("n_post", [D])
        n_fpre, n_fpost = self.din("n_fpre", [D]), self.din("n_fpost", [D])
        self.w_in = self.din("w_in", [D, INW])
        cqkv = self.din("cqkv", [4, 3 * 512])
        alog_d, dtb_d = self.din("a_log", [H]), self.din("dt_bias", [H])
        gdn_d = self.din("gdn", [DH])
        csc = self.din("csc", [3, 512])
        self.w_out = self.din("w_out", [D, D])
        self.w_gate, self.w_up = self.din("w_gate", [D, DFF]), self.din("w_up", [D, DFF])
        self.w_down = self.din("w_down", [DFF, D])
        self.out_d = nc.dram_tensor("out", [NROW, D], F32, kind="ExternalOutput").ap()
        self.d_dbg = S.dma_sem("dbg")
        self.d_const = S.dma_sem("const")

        self.banks = []
        for i in range(8):
            cm = nc.psum_tensor("bank%d" % i, [P, 512], F32)
            self.banks.append(cm.__enter__())
            self._keep.append(cm)

        sb = self.sb
        self.identf = sb("identf", [P, P])
        self.identb = sb("identb", [P, P], BF16)
        self.onesb = sb("onesb", [P, P], BF16)
        self.onesf = sb("onesf", [CH, P])
        self.U = sb("U", [CH, CH])
        self.SL8 = sb("SL8", [CH, 8, CH])
        self.I8f = sb("I8f", [CH, 8, CH])
        self.I8b = sb("I8b", [CH, 8, CH], BF16)
        self.NEG = [sb("NEG%d" % i, [CH, 8, CH]) for i in range(3)]
        tmp64 = sb("tmp64", [CH, CH])
        pl, dv = S.pool, S.dve
        S.op(pl, "memset", dict(ap=self.identf[:], constant=0.0), writes=["identf"])
        S.op(pl, "affine_select", dict(out=self.identf[:], in_=self.identf[:], pattern=[[-1, P]], compare_op=ALU.not_equal,
                                      fill=1.0, base=0, channel_multiplier=1), reads=["identf"], writes=["identf"])
        S.op(dv, "tensor_copy", dict(out=self.identb[:], in_=self.identf[:]), reads=["identf"], writes=["identb"])
        S.op(pl, "memset", dict(ap=self.onesb[:], constant=1.0), writes=["onesb"])
        S.op(pl, "memset", dict(ap=self.onesf[:], constant=1.0), writes=["onesf"])
        S.op(pl, "memset", dict(ap=self.U[:], constant=1.0), writes=["U"])
        S.op(pl, "affine_select", dict(out=self.U[:], in_=self.U[:], pattern=[[1, CH]], compare_op=ALU.is_ge, fill=0.0,
                                      base=0, channel_multiplier=-1), reads=["U"], writes=["U"])
        S.op(pl, "memset", dict(ap=tmp64[:], constant=1.0), writes=["tmp64"])
        S.op(pl, "affine_select", dict(out=tmp64[:], in_=tmp64[:], pattern=[[-1, CH]], compare_op=ALU.is_gt, fill=0.0,
                                      base=0, channel_multiplier=1), reads=["tmp64"], writes=["tmp64"])
        S.op(dv, "tensor_copy", dict(out=self.SL8[:], in_=tmp64[:].unsqueeze(1).to_broadcast([CH, 8, CH])),
             reads=["tmp64"], writes=["SL8"])
        S.op(dv, "tensor_copy", dict(out=self.I8f[:], in_=self.identf[0:CH, 0:CH].unsqueeze(1).to_broadcast([CH, 8, CH])),
             reads=["identf"], writes=["I8f"])
        S.op(dv, "tensor_copy", dict(out=self.I8b[:], in_=self.I8f[:]), reads=["I8f"], writes=["I8b"])
        specs = [([[-1, CH]], 1, ALU.is_gt), ([[1, CH]], -1, ALU.is_gt), ([[1, CH]], -1, ALU.is_ge)]
        for i, (pat, cm_, cop) in enumerate(specs):
            S.op(pl, "memset", dict(ap=tmp64[:], constant=0.0), reads=["tmp64"], writes=["tmp64"])
            S.op(pl, "affine_select", dict(out=tmp64[:], in_=tmp64[:], pattern=pat, compare_op=cop, fill=NEGM,
                                          base=0, channel_multiplier=cm_), reads=["tmp64"], writes=["tmp64"])
            S.op(dv, "tensor_copy", dict(out=self.NEG[i][:], in_=tmp64[:].unsqueeze(1).to_broadcast([CH, 8, CH])),
                 reads=["tmp64"], writes=["NEG%d" % i])

        self.NEGd = [sb("NEGd%d" % i, [NMETA, H * NMETA]) for i in range(3)]
        for i in range(3):
            S.op(dv, "tensor_copy", dict(out=self.NEGd[i][:].rearrange("p (b c) -> p b c", c=NMETA), in_=self.NEG[i][0:NMETA, 0:H, 0:NMETA]),
                 reads=["NEG%d" % i], writes=["NEGd%d" % i])
        self.wpre_col = sb("wpre_col", [P, KC])
        self.wf_col = sb("wf_col", [P, KC])
        self.wpost_b = sb("wpost_b", [P, D])
        self.wfpost_b = sb("wfpost_b", [P, D])
        self.gdnw = sb("gdnw", [P, 1])
        self.cwq = sb("cwq", [P, 12, 4])
        self.cws = sb("cws", [P, 4, 3])
        self.dtb = sb("dtb", [CH, H])
        self.negA = sb("negA", [CH, H])
        self.wlog = sb("wlog", [P, KC, 8], BF16)
        sp = S.sp
        dc = self.d_const
        S.dma(sp, dc, self.wpre_col[:], n_pre.rearrange("(k p) -> p k", p=P), writes=["wpre_col"], noncontig=True, defer=True)
        S.dma(sp, dc, self.wf_col[:], n_fpre.rearrange("(k p) -> p k", p=P), writes=["wf_col"], noncontig=True, defer=True)
        S.dma(sp, dc, self.wpost_b[:], n_post.partition_broadcast(P), writes=["wpost_b"], defer=True)
        S.dma(sp, dc, self.wfpost_b[:], n_fpost.partition_broadcast(P), writes=["wfpost_b"], defer=True)
        S.dma(sp, dc, self.gdnw[:], gdn_d.rearrange("(p o) -> p o", o=1), writes=["gdnw"], noncontig=True, defer=True)
        for c in range(12):
            S.dma(sp, dc, self.cwq[:, c, :], cqkv[:, c * P:(c + 1) * P].rearrange("k p -> p k"), writes=["cwq"], noncontig=True, defer=True)
        for c in range(4):
            S.dma(sp, dc, self.cws[:, c, :], csc[:, c * P:(c + 1) * P].rearrange("k p -> p k"), writes=["cws"], noncontig=True, defer=True)
        S.dma(sp, dc, self.dtb[:], dtb_d.partition_broadcast(CH), writes=["dtb"], defer=True)
        S.dma(sp, dc, self.negA[:], alog_d.partition_broadcast(CH), writes=["negA"], defer=True)
        S.flush(dc)
        S.op(S.act, "activation", dict(out=self.negA[:], in_=self.negA[:], func=AF.Exp), reads=["negA"], writes=["negA"])
        S.op(dv, "tensor_scalar", dict(out=self.negA[:], in0=self.negA[:], scalar1=-1.0, scalar2=None, op0=ALU.mult),
             reads=["negA"], writes=["negA"])
        S.dma(S.pool, S.dma_sem("wlog"), self.wlog[:], self.w_in[:, LGO:LGO + 8].rearrange("(k p) n -> p k n", p=P),
              writes=["wlog"], noncontig=True)

        self.wout = sb("wout", [P, KC, D], BF16)
        dwo = S.dma_sem("wout")
        for i in range(4):
            S.dma(S.pool, dwo, self.wout[:, 2 * i:2 * i + 2, :],
                  self.w_out[2 * i * P:(2 * i + 2) * P, :].rearrange("(k p) n -> p k n", p=P), writes=[("wout", i)], defer=True)
        S.flush(dwo)

        self.NRING = 4
        self.wring = [sb("wring%d" % i, [P, 2048], BF16) for i in range(self.NRING)]
        self.wring_sem = [S.dma_sem("wr%d" % i) for i in range(self.NRING)]
        self.pieces = self.make_pieces()
        self.piece_issued = 0

        self.xh = [sb("xh0", [P, NSUB, D])] * 2
        self.xh_ld = [S.dma_sem("xld0")] * 2
        self.xh_st = [S.dma_sem("xst0")] * 2
        self.xm = sb("xm", [NMETA, D])
        self.uT = sb("uT", [P, KC, COLS], BF16)
        self.Sst = sb("Sst", [P, H, DH])
        self.Sbf = sb("Sbf", [P, H, DH], BF16)
        self.Smeta = sb("Smeta", [P, H, DH])
        self.ptq = sb("ptq", [P, 12, 3])
        self.mtq = sb("mtq", [P, 12, 3])
        self.pts = sb("pts", [P, 4, 2])
        self.mts = sb("mts", [P, 4, 2])

        self.xs_ring = self.ring("xs", 1, [P, D], BF16)
        self.stat = sb("stat", [P, 16])
        self.colpool = self.ring("col", 9, [P, COLS + 4], F32)
        self.sqr = self.ring("sq", 4, [P, COLS], BF16)
        self.yT = sb("yT", [P, 8, COLS], BF16)
        self.g64f = self.ring("g64f", 8, [CH, 8, CH], F32)
        self.g64f.update(manual=True, free=list(range(8)))
        self.g64b = self.ring("g64b", 13, [CH, 8, CH], BF16)
        self.g64b.update(manual=True, free=list(range(13)))
        self.kbgr = self.ring("kbg", 2, [CH, 8, DH], BF16)
        self.t128 = self.ring("t128", 4, [P, 512], F32)
        self.osq = sb("osq", [P, 512], BF16)
        self.vnew = sb("vnew", [CH, H, DH], BF16)
        NS = NCH + 1
        self.NS = NS
        self.gt = {n: sb("gt_" + n, [CH, NS, H]) for n in
                   ["t1", "t2", "lnb", "beta", "x2", "m", "r", "g", "egc", "ekd", "gcs", "bge"]}
        self.gl = sb("gl", [P, NS, H])

        HB = (NCH // 2) * H
        self.HB = HB
        sizes_mix = [("kdec", HB * DH // 2), ("vb", HB * DH // 2), ("u", HB * DH), ("wT", HB * CH // 2), ("QKT", HB * CH // 2),
                     ("qnT", H * COLS // 2), ("knT", H * COLS // 2), ("vT", H * COLS // 2), ("qdT", H * COLS // 2),
                     ("zsT", H * COLS // 2)]
        sizes_ffn = [("aT", FC * T // 2), ("dn", NSUB * D)]
        tot_mix = sum(s for _, s in sizes_mix)
        tot_ffn = sum(s for _, s in sizes_ffn)
        self.arena = sb("arena", [P, max(tot_mix, tot_ffn)])
        off = 0
        av = {}
        for n, s in sizes_mix:
            av[n] = self.arena[:, off:off + s]
            off += s
        off = 0
        for n, s in sizes_ffn:
            av[n] = self.arena[:, off:off + s]
            off += s
        self.kdec = av["kdec"][0:CH, :].bitcast(BF16).rearrange("p (b d) -> p b d", d=DH)
        self.vb = av["vb"][0:CH, :].bitcast(BF16).rearrange("p (b d) -> p b d", d=DH)
        self.u = av["u"][0:CH, :].rearrange("p (b d) -> p b d", d=DH)
        self.wT = av["wT"].bitcast(BF16).rearrange("p (b c) -> p b c", c=CH)
        self.QKT = av["QKT"][0:CH, :].bitcast(BF16).rearrange("p (b c) -> p b c", c=CH)
        self.qnT = av["qnT"].bitcast(BF16).rearrange("p (h c) -> p h c", h=H)
        self.knT = av["knT"].bitcast(BF16).rearrange("p (h c) -> p h c", h=H)
        self.vT = av["vT"].bitcast(BF16).rearrange("p (h c) -> p h c", h=H)
        self.qdT = av["qdT"].bitcast(BF16).rearrange("p (h c) -> p h c", h=H)
        self.zsT = av["zsT"].bitcast(BF16).rearrange("p (h c) -> p h c", h=H)
        self.aT = av["aT"].bitcast(BF16).rearrange("p (f t) -> p f t", f=FC)
        self.dn = av["dn"].rearrange("p (s d) -> p s d", s=NSUB)
        self.mix_keys = ["kdec", "vb", "u", "wT", "QKT", "qnT", "knT", "vT", "qdT", "zsT"]
        self.ffn_keys = ["aT", "dn"]

        ntiles = cfg.nseq * cfg.ntile
        for ti in range(ntiles):
            self.tile(ti)
        self.sbuf_left = nc.sbuf_bytes_remaining
        S.final_wait(S.sp, [self.xh_st[0], self.d_dbg])
        S.emit()

    def chk(self, tag, cond=True):
        if cond and self.stop == tag:
            raise _Stop()

    def mmw(self, out, lhsT, rhs, start, stop, reads, writes):
        n = out.shape[-1]
        for c0 in range(0, n, self.FN):
            c1 = min(n, c0 + self.FN)
            self.S.op(self.S.pe, "matmul", dict(out=out[:, c0:c1], lhsT=lhsT, rhs=rhs[:, c0:c1], start=start, stop=stop),
                      reads=reads, writes=writes)

    def make_pieces(self):
        cfg = self.cfg
        pieces = []
        for ti in range(cfg.nseq * cfg.ntile):
            offs = [QO + 256 * i for i in range(8)]
            offs += [SXO, SCO, SBO, SXO + 256, SCO + 256, SBO + 256]
            for o in offs:
                pieces.append(("in", self.w_in[:, o:o + 256].rearrange("(k p) n -> p k n", p=P), "k8"))
            for j in range(11):
                pieces.append(("gate", self.w_gate[:, 256 * j:256 * j + 256].rearrange("(k p) n -> p k n", p=P), "k8"))
                pieces.append(("up", self.w_up[:, 256 * j:256 * j + 256].rearrange("(k p) n -> p k n", p=P), "k8"))
            for oh in range(2):
                for g in range(6):
                    k0, k1 = 4 * g, min(4 * g + 4, FC)
                    pieces.append(("down", self.w_down[k0 * P:k1 * P, oh * 512:(oh + 1) * 512].rearrange("(k p) n -> p k n", p=P),
                                   ("k4", k1 - k0)))
        return pieces

    def next_piece(self, expect):
        S = self.S
        idx = self.piece_cons if hasattr(self, "piece_cons") else 0
        self.piece_cons = idx + 1
        while self.piece_issued < min(len(self.pieces), idx + self.NRING - 1):
            j = self.piece_issued
            kind, src, vs = self.pieces[j]
            slot = j % self.NRING
            if vs == "k8":
                dst = self.wring[slot][:].rearrange("p (k n) -> p k n", k=8)
            else:
                dst = self.wring[slot][:].rearrange("p (k n) -> p k n", k=4)[:, 0:vs[1], :]
            S.dma(S.pool, self.wring_sem[slot], dst, src, writes=[("wring", slot)])
            self.piece_issued += 1
        kind, src, vs = self.pieces[idx]
        assert kind == expect, (kind, expect)
        slot = idx % self.NRING
        if vs == "k8":
            view = self.wring[slot][:].rearrange("p (k n) -> p k n", k=8)
        else:
            view = self.wring[slot][:].rearrange("p (k n) -> p k n", k=4)
        return view, ("wring", slot)

    def prenorm_T(self, xh, xkey, wcol, wkey, with_meta):
        cfg, S = self.cfg, self.S
        NSUB, T = cfg.nsub, cfg.T
        st = self.stat
        xs_l = []
        jk, jkk = self.take(self.xs_ring)
        for s4 in range(NSUB):
            S.op(S.act, "activation", dict(out=jk[:], in_=xh[:, s4, :], func=AF.Square, accum_out=st[:, s4:s4 + 1]),
                 reads=[xkey], writes=[jkk, "stat"])
        S.op(S.act, "activation", dict(out=st[:, 4:4 + NSUB], in_=st[:, 0:NSUB], func=AF.Ln, scale=1.0 / D, bias=EPS),
             reads=["stat"], writes=["stat"])
        S.op(S.act, "activation", dict(out=st[:, 8:8 + NSUB], in_=st[:, 4:4 + NSUB], func=AF.Exp, scale=-0.5),
             reads=["stat"], writes=["stat"])
        if with_meta:
            S.op(S.act, "activation", dict(out=jk[0:NMETA, :], in_=self.xm[:], func=AF.Square,
                                           accum_out=st[0:NMETA, 12:13]), reads=["xm"], writes=[jkk, "stat"])
            S.op(S.act, "activation", dict(out=st[0:NMETA, 13:14], in_=st[0:NMETA, 12:13], func=AF.Ln, scale=1.0 / D, bias=EPS),
                 reads=["stat"], writes=["stat"])
            S.op(S.act, "activation", dict(out=st[0:NMETA, 14:15], in_=st[0:NMETA, 13:14], func=AF.Exp, scale=-0.5),
                 reads=["stat"], writes=["stat"])
        xs_all = []
        for s4 in range(NSUB):
            cb, cbk = self.take(self.colpool)
            xs = cb[:].bitcast(BF16)[:, 0:D]
            S.op(S.act, "activation", dict(out=xs, in_=xh[:, s4, :], func=AF.Copy, scale=st[:, 8 + s4:9 + s4]),
                 reads=[xkey, "stat"], writes=[cbk])
            xs_all.append((xs, cbk, s4))
        for kc in range(KC):
            bk, bkey = self.bank()
            bt = bk.bitcast(BF16)
            for j, (xs, cbk, s4) in enumerate(xs_all):
                S.op(S.pe, "transpose", dict(out=bt[:, j * P:(j + 1) * P], in_=xs[:, kc * P:(kc + 1) * P],
                                             identity=self.identb[:]), reads=[cbk, "identb"], writes=[bkey])
            S.op(S.dve, "tensor_scalar", dict(out=self.uT[:, kc, NMETA:NMETA + NSUB * P], in0=bt[:, 0:NSUB * P],
                                              scalar1=wcol[:, kc:kc + 1], scalar2=None, op0=ALU.mult),
                 reads=[bkey, wkey], writes=["uT"])
        if with_meta:
            xs, xk = self.take(self.xs_ring)
            S.op(S.act, "activation", dict(out=xs[0:NMETA, :], in_=self.xm[:], func=AF.Copy, scale=st[0:NMETA, 14:15]),
                 reads=["xm", "stat"], writes=[xk])
            bk, bkey = self.bank()
            bt = bk.bitcast(BF16)
            for kc in range(KC):
                S.op(S.pe, "transpose", dict(out=bt[:, kc * NMETA:(kc + 1) * NMETA], in_=xs[0:NMETA, kc * P:(kc + 1) * P],
                                             identity=self.identb[0:NMETA, 0:NMETA]), reads=[xk, "identb"], writes=[bkey])
            for kc in range(KC):
                S.op(S.dve, "tensor_scalar", dict(out=self.uT[:, kc, 0:NMETA], in0=bt[:, kc * NMETA:(kc + 1) * NMETA],
                                                  scalar1=wcol[:, kc:kc + 1], scalar2=None, op0=ALU.mult),
                     reads=[bkey, wkey], writes=["uT"])

    def proj_fm(self, wview, wkey, oc, splits):
        S = self.S
        res = []
        for (n0, n1) in splits:
            bk, bkey = self.bank(getattr(self, "fm_pool", None))
            for kc in range(KC):
                S.op(S.pe, "matmul", dict(out=bk[:, 0:n1 - n0], lhsT=wview[:, kc, oc * P:(oc + 1) * P], rhs=self.uT[:, kc, n0:n1],
                                          start=(kc == 0), stop=(kc == KC - 1)), reads=[wkey, "uT"], writes=[bkey])
            res.append((bk, bkey, n0, n1))
        return res

    def tile(self, ti):
        try:
            self.tile_(ti)
        except _Stop:
            cfg = self.cfg
            r0 = (ti // cfg.ntile) * cfg.seq + (ti % cfg.ntile) * cfg.T
            self.finish_tile(ti, self.xh[0], ("xh", 0), r0)

    def tile_(self, ti):
        cfg, S = self.cfg, self.S
        T, COLS, NSUB, NCH = cfg.T, cfg.cols, cfg.nsub, cfg.nch
        seq_i, j = ti // cfg.ntile, ti % cfg.ntile
        first_ever = (ti == 0)
        seq_start = (j == 0)
        r0 = seq_i * cfg.seq + j * T
        xh = self.xh[ti % 2]
        xkey = ("xh", 0)
        if ti > 0:
            S.alias(self.ffn_keys, self.mix_keys)
        S.dma(S.sp, self.xh_ld[ti % 2], xh[:], self.x_d[r0:r0 + T, :].rearrange("(s p) d -> p s d", p=P), writes=[xkey])
        if first_ever:
            S.dma(S.sp, S.dma_sem("xm"), self.xm[:], self.meta_d, writes=["xm"])
        c0 = 0 if first_ever else NMETA
        splits = ([(0, NMETA)] if first_ever else []) + [(NMETA + 512 * i, NMETA + 512 * (i + 1)) for i in range(T // 512)]
        rsplits = [(NMETA + 512 * i, NMETA + 512 * (i + 1)) for i in range(T // 512)]

        self.prenorm_T(xh, xkey, self.wpre_col, "wpre_col", first_ever)
        self.dump("uT", self.uT[:], "uT", [P, KC, COLS], BF16)

        if self.stop == "A":
            return self.finish_tile(ti, xh, xkey, r0)
        chunks = [(NMETA + CH * c, CH, c) for c in range(NCH)]
        if first_ever:
            chunks = [(0, NMETA, NCH)] + chunks
        plg, plg_key = self.bank({"b": [6], "i": 0})
        for (col0, C, slot) in chunks:
            for kc in range(KC):
                S.op(S.pe, "matmul", dict(out=plg[0:C, slot * 8:slot * 8 + 8], lhsT=self.uT[:, kc, col0:col0 + C],
                                          rhs=self.wlog[:, kc, :], start=(kc == 0), stop=(kc == KC - 1)),
                     reads=["uT", "wlog"], writes=[plg_key])
        if first_ever:
            for _ in self.gating(plg, plg_key, NMETA, NCH, 1):
                pass
        gate_gens = [self.gating(plg, plg_key, CH, 0, NCH, {"b": [7], "i": 0})]
        if self.stop == "C" or "g" in self.dbg:
            for gg in gate_gens:
                for _ in gg:
                    pass
            gate_gens = []
        self.dump("g", self.gt["g"][:], "gt_g", [CH, self.NS, H])
        self.dump("beta", self.gt["beta"][:], "gt_beta", [CH, self.NS, H])

        if self.stop == "C":
            return self.finish_tile(ti, xh, xkey, r0)
        active = list(gate_gens)

        def step_all(n):
            for _ in range(n):
                for g_ in list(active):
                    try:
                        next(g_)
                    except StopIteration:
                        active.remove(g_)
        for pi in range(8):
            wv, wkey = self.next_piece("in")
            for o2 in range(2):
                oc = pi * 2 + o2
                if oc < 12:
                    accs = self.proj_fm(wv, wkey, o2, splits)
                    active.append(self.evac_qkv(oc, accs, ti, c0))
                    step_all(self.QSTEP)
                else:
                    step_all(self.QSTEP)
                    accs = self.proj_fm(wv, wkey, o2, rsplits)
                    for (bk, bkey, n0, n1) in accs:
                        S.op(S.act, "activation", dict(out=self.zsT[:, oc - 12, n0:n1], in_=bk[:, 0:n1 - n0], func=AF.Silu),
                             reads=[bkey], writes=["zsT"])
        while active:
            step_all(1)
        self.dump("qnT", self.qnT, "qnT", [P, H, COLS], BF16)
        self.dump("knT", self.knT, "knT", [P, H, COLS], BF16)
        self.dump("vT", self.vT, "vT", [P, H, COLS], BF16)
        if self.stop == "B":
            return self.finish_tile(ti, xh, xkey, r0)
        def sc_gen():
            self.fm_pool = {"b": [4, 5], "i": 0}
            for half in range(2):
                sx = {}
                cv = {}
                wv, wkey = self.next_piece("in")
                for o2 in range(2):
                    cc = half * 2 + o2
                    accs = self.proj_fm(wv, wkey, o2, splits)
                    t, tk = self.take(self.colpool)
                    for (bk, bkey, n0, n1) in accs:
                        S.op(S.act, "activation", dict(out=t[:, n0:n1], in_=bk[:, 0:n1 - n0], func=AF.Copy), reads=[bkey], writes=[tk])
                    sx[cc] = (t, tk)
                    yield
                wv, wkey = self.next_piece("in")
                for o2 in range(2):
                    cc = half * 2 + o2
                    accs = self.proj_fm(wv, wkey, o2, splits)
                    raw, rk = self.take(self.colpool)
                    t, tk = sx[cc]
                    for (bk, bkey, n0, n1) in accs:
                        S.op(S.dve, "tensor_tensor", dict(out=raw[:, 2 + n0:2 + n1], in0=bk[:, 0:n1 - n0], in1=t[:, n0:n1], op=ALU.mult),
                             reads=[bkey, tk], writes=[rk])
                    yield
                    if first_ever:
                        S.op(S.dve, "memset", dict(ap=raw[:, 0:2], constant=0.0), writes=[rk])
                        S.op(S.dve, "tensor_copy", dict(out=self.mts[:, cc, :], in_=raw[:, 2 + NMETA - 2:2 + NMETA]), reads=[rk], writes=["mts"])
                    elif seq_start:
                        S.op(S.dve, "tensor_copy", dict(out=raw[:, NMETA:NMETA + 2], in_=self.mts[:, cc, :]), reads=["mts"], writes=[rk])
                    else:
                        S.op(S.dve, "tensor_copy", dict(out=raw[:, NMETA:NMETA + 2], in_=self.pts[:, cc, :]), reads=["pts"], writes=[rk])
                    S.op(S.dve, "tensor_copy", dict(out=self.pts[:, cc, :], in_=raw[:, 2 + COLS - 2:2 + COLS]), reads=[rk], writes=["pts"])
                    cvt, ck = self.take(self.colpool)
                    S.op(S.dve, "tensor_scalar", dict(out=cvt[:, c0:COLS], in0=raw[:, 2 + c0:2 + COLS], scalar1=self.cws[:, cc, 2:3],
                                                      scalar2=None, op0=ALU.mult), reads=[rk, "cws"], writes=[ck])
                    yield
                    for tap in (1, 0):
                        sh = 2 - tap
                        S.op(S.dve, "scalar_tensor_tensor", dict(out=cvt[:, c0:COLS], in0=raw[:, 2 + c0 - sh:2 + COLS - sh],
                                                                 scalar=self.cws[:, cc, tap:tap + 1], in1=cvt[:, c0:COLS],
                                                                 op0=ALU.mult, op1=ALU.add), reads=[rk, ck, "cws"], writes=[ck])
                        yield
                    cv[cc] = (cvt, ck)
                wv, wkey = self.next_piece("in")
                for o2 in range(2):
                    cc = half * 2 + o2
                    accs = self.proj_fm(wv, wkey, o2, rsplits)
                    cvt, ck = cv[cc]
                    for (bk, bkey, n0, n1) in accs:
                        S.op(S.dve, "tensor_tensor", dict(out=self.yT[:, 4 + cc, n0:n1], in0=bk[:, 0:n1 - n0], in1=cvt[:, n0:n1], op=ALU.mult),
                             reads=[bkey, ck], writes=["yT"])
                    yield

        def e_gen(s4, ebp):
            accs = []
            for oh in range(2):
                bk, bkey = self.bank(ebp)
                for kc in range(KC):
                    S.op(S.pe, "matmul", dict(out=bk[:, :], lhsT=self.yT[:, kc, NMETA + s4 * P:NMETA + (s4 + 1) * P],
                                              rhs=self.wout[:, kc, oh * 512:(oh + 1) * 512], start=(kc == 0), stop=(kc == KC - 1)),
                         reads=["yT", ("wout", kc // 2)], writes=[bkey])
                accs.append((bk, bkey))
                yield
            self.postnorm_add(accs, xh, xkey, s4, self.wpost_b, "wpost_b")
            yield

        HCH = NCH // 2

        def chain_gen(hf):
            for cp in range(HCH // 2):
                ob, okey = self.bank({"b": [7], "i": 0})
                for i in range(2):
                    cl = 2 * cp + i
                    c = hf * HCH + cl
                    yield from self.chain_chunk(NMETA + CH * c, CH, c, cl * 4, (ob, okey, i))
                yield from self.o_norm(ob, okey, NMETA + CH * (hf * HCH + 2 * cp))

        def preps(hf):
            gl_ = []
            for g in range(HCH // 2):
                cbase = hf * HCH + 2 * g
                grp = [(NMETA + CH * (cbase + i), CH, cbase + i) for i in range(2)]
                gl_.append(self.delta_prep(grp, g * 8, {"b": [2 * g, 2 * g + 1], "i": 0}))
            return gl_

        def run(gens, until=None, hold=None):
            gens = list(gens)
            must = list(until) if until is not None else list(gens)
            parked = []
            while must:
                for g_ in list(gens):
                    if g_ in parked:
                        if hold is not None and hold in gens:
                            continue
                        parked.remove(g_)
                    try:
                        r_ = next(g_)
                    except StopIteration:
                        gens.remove(g_)
                        if g_ in must:
                            must.remove(g_)
                        continue
                    if r_ == "SYNC" and hold is not None and hold in gens:
                        parked.append(g_)
            return gens

        if self.stop == "D":
            for _ in sc_gen():
                pass
            return self.finish_tile(ti, xh, xkey, r0)
        if first_ever:
            S.op(S.dve, "memset", dict(ap=self.Sst[:], constant=0.0), writes=["Sst"])
            S.op(S.dve, "memset", dict(ap=self.Sbf[:], constant=0.0), writes=["Sbf"])
            for _ in self.delta_prep([(0, NMETA, NCH)], 0):
                pass
            for _ in self.chain_chunk(0, NMETA, NCH, 0, None):
                pass
            self.dump("Sm", self.Sst[:], "Sst", [P, H, DH])
            S.op(S.act, "activation", dict(out=self.Smeta[:], in_=self.Sst[:], func=AF.Copy), reads=["Sst"], writes=["Smeta"])
        elif seq_start:
            S.op(S.act, "activation", dict(out=self.Sst[:], in_=self.Smeta[:], func=AF.Copy), reads=["Smeta"], writes=["Sst"])
            S.op(S.act, "activation", dict(out=self.Sbf[:], in_=self.Smeta[:], func=AF.Copy), reads=["Smeta"], writes=["Sbf"])
        scg = sc_gen()
        p0 = preps(0)
        rest = run([scg] + p0, until=p0)
        c0g = chain_gen(0)
        p1 = preps(1)
        rest = run(rest + [c0g] + p1, hold=c0g)
        c1g = chain_gen(1)
        self.fm_pool = None
        egs = [e_gen(s4, {"b": [2 * (s4 % 2), 2 * (s4 % 2) + 1], "i": 0}) for s4 in range(NSUB // 2)]
        run([c1g] + egs)
        self.dump("yT", self.yT[:], "yT", [P, 8, COLS], BF16)
        self.dump("S", self.Sst[:], "Sst", [P, H, DH])
        if self.stop == "DELTA":
            return self.finish_tile(ti, xh, xkey, r0)
        run([e_gen(s4, {"b": [2 * (s4 % 2), 2 * (s4 % 2) + 1], "i": 0}) for s4 in range(NSUB // 2, NSUB)])
        self.dump("h1", xh[:], xkey, [P, NSUB, D])

        if self.stop == "E":
            return self.finish_tile(ti, xh, xkey, r0)
        S.alias(self.mix_keys, self.ffn_keys)
        self.rr_n = 8
        self.prenorm_T(xh, xkey, self.wf_col, "wf_col", False)
        for jp in range(11):
            wg, wgk = self.next_piece("gate")
            wu, wuk = self.next_piece("up")
            for o2 in range(2):
                fc = 2 * jp + o2
                ga = self.proj_fm(wg, wgk, o2, rsplits)
                ua = self.proj_fm(wu, wuk, o2, rsplits)
                for (gb, gk, n0, n1), (ub, uk, _, _) in zip(ga, ua):
                    sg, sk = self.take(self.t128)
                    S.op(S.act, "activation", dict(out=sg[:, 0:n1 - n0], in_=gb[:, 0:n1 - n0], func=AF.Silu), reads=[gk], writes=[sk])
                    S.op(S.dve, "tensor_tensor", dict(out=self.aT[:, fc, n0 - NMETA:n1 - NMETA], in0=ub[:, 0:n1 - n0], in1=sg[:, 0:n1 - n0],
                                                      op=ALU.mult), reads=[uk, sk], writes=["aT"])
        self.dump("u2T", self.uT[:], "uT", [P, KC, COLS], BF16)
        self.dump("aT", self.aT, "aT", [P, FC, T], BF16)
        for oh in range(2):
            accl = [self.bank() for _ in range(NSUB)]
            for g in range(6):
                wv, wk = self.next_piece("down")
                k0, k1 = 4 * g, min(4 * g + 4, FC)
                for s4 in range(NSUB):
                    bk, bkey = accl[s4]
                    for kc in range(k0, k1):
                        S.op(S.pe, "matmul", dict(out=bk[:, :], lhsT=self.aT[:, kc, s4 * P:(s4 + 1) * P], rhs=wv[:, kc - k0, :],
                                                  start=(kc == 0), stop=(kc == FC - 1)), reads=["aT", wk], writes=[bkey])
            for s4 in range(NSUB):
                bk, bkey = accl[s4]
                S.op(S.act, "activation", dict(out=self.dn[:, s4, oh * 512:(oh + 1) * 512], in_=bk[:, :], func=AF.Copy),
                     reads=[bkey], writes=["dn"])
        self.dump("dn", self.dn, "dn", [P, NSUB, D])
        self.postnorm_all(xh, xkey)
        self.rr_n = 6
        self.finish_tile(ti, self.dn, "dn", r0)

    def finish_tile(self, ti, xh, xkey, r0):
        S, T = self.S, self.cfg.T
        src = xh[:] if not isinstance(xh, bass.AP) else xh
        S.dma(S.sp, self.xh_st[ti % 2], self.out_d[r0:r0 + T, :].rearrange("(s p) d -> p s d", p=P), src, reads=[xkey])

    def postnorm_add(self, accs, xh, xkey, s4, wb, wbkey, dst=None):
        S = self.S
        st = self.stat
        jk, jkk = self.take(self.xs_ring)
        for oh, (src, skey) in enumerate(accs):
            S.op(S.act, "activation", dict(out=jk[:, 0:512], in_=src, func=AF.Square, accum_out=st[:, oh:oh + 1]),
                 reads=[skey], writes=[jkk, "stat"])
        S.op(S.dve, "tensor_tensor", dict(out=st[:, 2:3], in0=st[:, 0:1], in1=st[:, 1:2], op=ALU.add), reads=["stat"], writes=["stat"])
        S.op(S.act, "activation", dict(out=st[:, 3:4], in_=st[:, 2:3], func=AF.Ln, scale=1.0 / D, bias=EPS), reads=["stat"], writes=["stat"])
        S.op(S.act, "activation", dict(out=st[:, 4:5], in_=st[:, 3:4], func=AF.Exp, scale=-0.5), reads=["stat"], writes=["stat"])
        for oh, (src, skey) in enumerate(accs):
            t, tk = self.take(self.t128)
            S.op(S.dve, "scalar_tensor_tensor", dict(out=t[:], in0=src, scalar=st[:, 4:5], in1=wb[:, oh * 512:(oh + 1) * 512],
                                                     op0=ALU.mult, op1=ALU.mult), reads=[skey, "stat", wbkey], writes=[tk])
            dt_, dk_ = dst if dst is not None else (xh, xkey)
            S.op(S.dve, "tensor_tensor", dict(out=dt_[:, s4, oh * 512:(oh + 1) * 512], in0=xh[:, s4, oh * 512:(oh + 1) * 512], in1=t[:],
                                              op=ALU.add), reads=[tk, xkey], writes=[dk_])

    def postnorm_all(self, xh, xkey):
        S, NSUB = self.S, self.cfg.nsub
        st = self.stat
        jk, jkk = self.take(self.xs_ring)
        for s4 in range(NSUB):
            S.op(S.act, "activation", dict(out=jk[:], in_=self.dn[:, s4, :], func=AF.Square, accum_out=st[:, s4:s4 + 1]),
                 reads=["dn"], writes=[jkk, "stat"])
        S.op(S.act, "activation", dict(out=st[:, 4:4 + NSUB], in_=st[:, 0:NSUB], func=AF.Ln, scale=1.0 / D, bias=EPS),
             reads=["stat"], writes=["stat"])
        S.op(S.act, "activation", dict(out=st[:, 8:8 + NSUB], in_=st[:, 4:4 + NSUB], func=AF.Exp, scale=-0.5),
             reads=["stat"], writes=["stat"])
        for s4 in range(NSUB):
            for oh in range(2):
                t, tk = self.take(self.t128)
                sl = slice(oh * 512, (oh + 1) * 512)
                S.op(S.dve, "scalar_tensor_tensor", dict(out=t[:], in0=self.dn[:, s4, sl], scalar=st[:, 8 + s4:9 + s4], in1=self.wfpost_b[:, sl],
                                                         op0=ALU.mult, op1=ALU.mult), reads=["dn", "stat", "wfpost_b"], writes=[tk])
                S.op(S.dve, "tensor_tensor", dict(out=self.dn[:, s4, sl], in0=xh[:, s4, sl], in1=t[:], op=ALU.add),
                     reads=[tk, xkey], writes=["dn"])

    def evac_qkv(self, oc, accs, ti, c0):
        cfg, S = self.cfg, self.S
        COLS = cfg.cols
        first_ever = (ti == 0)
        seq_start = (ti % cfg.ntile == 0)
        raw, rk = self.take(self.colpool)
        sq, sqk = self.take(self.sqr)
        for (bk, bkey, n0, n1) in accs:
            S.op(S.act, "activation", dict(out=raw[:, 3 + n0:3 + n1], in_=bk[:, 0:n1 - n0], func=AF.Copy), reads=[bkey], writes=[rk])
        if first_ever:
            S.op(S.dve, "memset", dict(ap=raw[:, 0:3], constant=0.0), writes=[rk])
            S.op(S.dve, "tensor_copy", dict(out=self.mtq[:, oc, :], in_=raw[:, 3 + NMETA - 3:3 + NMETA]), reads=[rk], writes=["mtq"])
        elif seq_start:
            S.op(S.dve, "tensor_copy", dict(out=raw[:, NMETA:NMETA + 3], in_=self.mtq[:, oc, :]), reads=["mtq"], writes=[rk])
        else:
            S.op(S.dve, "tensor_copy", dict(out=raw[:, NMETA:NMETA + 3], in_=self.ptq[:, oc, :]), reads=["ptq"], writes=[rk])
        S.op(S.dve, "tensor_copy", dict(out=self.ptq[:, oc, :], in_=raw[:, 3 + COLS - 3:3 + COLS]), reads=[rk], writes=["ptq"])
        yield
        cv, ck = self.take(self.colpool)
        S.op(S.dve, "tensor_scalar", dict(out=cv[:, c0:COLS], in0=raw[:, 3 + c0:3 + COLS], scalar1=self.cwq[:, oc, 3:4], scalar2=None,
                                          op0=ALU.mult), reads=[rk, "cwq"], writes=[ck])
        yield
        for tap in (2, 1, 0):
            sh = 3 - tap
            S.op(S.dve, "scalar_tensor_tensor", dict(out=cv[:, c0:COLS], in0=raw[:, 3 + c0 - sh:3 + COLS - sh],
                                                     scalar=self.cwq[:, oc, tap:tap + 1], in1=cv[:, c0:COLS], op0=ALU.mult, op1=ALU.add),
                 reads=[rk, ck, "cwq"], writes=[ck])
            yield
        h = oc % 4
        if oc >= 8:
            S.op(S.act, "activation", dict(out=self.vT[:, h, c0:COLS], in_=cv[:, c0:COLS], func=AF.Silu), reads=[ck], writes=["vT"])
            return
        qs, qk = self.take(self.colpool)
        S.op(S.act, "activation", dict(out=qs[:, c0:COLS], in_=cv[:, c0:COLS], func=AF.Silu), reads=[ck], writes=[qk])
        yield
        if self.POW:
            qsc = float(np.sqrt(128.0)) if oc < 4 else 1.0
            epsq = EPS * (128.0 if oc < 4 else 1.0)
            S.op(S.act, "activation", dict(out=sq[:, c0:COLS], in_=qs[:, c0:COLS], func=AF.Square, scale=qsc), reads=[qk], writes=[sqk])
            yield
            rq, rqk = self.take(self.colpool)
            for (n0, n1) in ([(0, NMETA)] if first_ever else []) + [(NMETA + 512 * i, NMETA + 512 * (i + 1)) for i in range(cfg.T // 512)]:
                bk, bkey = self.bank()
                S.op(S.pe, "matmul", dict(out=bk[:, 0:n1 - n0], lhsT=self.onesb[:], rhs=sq[:, n0:n1], start=True, stop=True),
                     reads=["onesb", sqk], writes=[bkey])
                S.op(S.dve, "tensor_scalar", dict(out=rq[:, n0:n1], in0=bk[:, 0:n1 - n0], scalar1=epsq, scalar2=None, op0=ALU.add),
                     reads=[bkey], writes=[rqk])
            yield
            S.op(S.pool, "tensor_scalar", dict(out=rq[:, c0:COLS], in0=rq[:, c0:COLS], scalar1=-0.5, scalar2=None, op0=ALU.pow),
                 reads=[rqk], writes=[rqk])
            yield
        else:
            S.op(S.act, "activation", dict(out=sq[:, c0:COLS], in_=qs[:, c0:COLS], func=AF.Square), reads=[qk], writes=[sqk])
            yield
            rq, rqk = self.take(self.colpool)
            for (n0, n1) in ([(0, NMETA)] if first_ever else []) + [(NMETA + 512 * i, NMETA + 512 * (i + 1)) for i in range(cfg.T // 512)]:
                bk, bkey = self.bank()
                S.op(S.pe, "matmul", dict(out=bk[:, 0:n1 - n0], lhsT=self.onesb[:], rhs=sq[:, n0:n1], start=True, stop=True),
                     reads=["onesb", sqk], writes=[bkey])
                S.op(S.act, "activation", dict(out=rq[:, n0:n1], in_=bk[:, 0:n1 - n0], func=AF.Ln, bias=EPS), reads=[bkey], writes=[rqk])
            yield
            if oc < 4:
                S.op(S.dve, "tensor_scalar", dict(out=rq[:, c0:COLS], in0=rq[:, c0:COLS], scalar1=float(np.log(128.0)), scalar2=None, op0=ALU.add),
                     reads=[rqk], writes=[rqk])
            S.op(S.act, "activation", dict(out=rq[:, c0:COLS], in_=rq[:, c0:COLS], func=AF.Exp, scale=-0.5), reads=[rqk], writes=[rqk])
            yield
        dst, dkey = (self.qnT, "qnT") if oc < 4 else (self.knT, "knT")
        S.op(S.dve, "tensor_tensor", dict(out=dst[:, h, c0:COLS], in0=qs[:, c0:COLS], in1=rq[:, c0:COLS], op=ALU.mult),
             reads=[qk, rqk], writes=[dkey])

    def gating(self, plg, plg_key, C, s0, n, gbp=None):
        S = self.S
        gt = self.gt
        lg = plg[0:C, s0 * 8:(s0 + n) * 8].rearrange("p (s e) -> p s e", e=8)
        b_ap, a_ap = lg[:, :, 0:4], lg[:, :, 4:8]

        def v(name):
            return gt[name][0:C, s0:s0 + n, :]
        A, Dv = S.act, S.dve
        S.op(A, "activation", dict(out=v("t1"), in_=b_ap, func=AF.Exp, scale=-1.0), reads=[plg_key], writes=["gt_t1"])
        yield
        S.op(A, "activation", dict(out=v("t2"), in_=v("t1"), func=AF.Ln, bias=1.0), reads=["gt_t1"], writes=["gt_t2"])
        yield
        S.op(Dv, "tensor_scalar", dict(out=v("lnb"), in0=v("t2"), scalar1=-1.0, scalar2=None, op0=ALU.mult), reads=["gt_t2"], writes=["gt_lnb"])
        yield
        S.op(A, "activation", dict(out=v("beta"), in_=v("t2"), func=AF.Exp, scale=-1.0), reads=["gt_t2"], writes=["gt_beta"])
        yield
        S.op(Dv, "tensor_tensor", dict(out=v("x2"), in0=a_ap, in1=self.dtb[0:C, :].unsqueeze(1).to_broadcast([C, n, H]), op=ALU.add),
             reads=[plg_key, "dtb"], writes=["gt_x2"])
        yield
        S.op(Dv, "tensor_scalar", dict(out=v("m"), in0=v("x2"), scalar1=-1.0, scalar2=None, op0=ALU.mult), reads=["gt_x2"], writes=["gt_m"])
        yield
        S.op(Dv, "tensor_tensor", dict(out=v("m"), in0=v("m"), in1=v("x2"), op=ALU.min), reads=["gt_m", "gt_x2"], writes=["gt_m"])
        yield
        S.op(A, "activation", dict(out=v("m"), in_=v("m"), func=AF.Exp), reads=["gt_m"], writes=["gt_m"])
        yield
        S.op(A, "activation", dict(out=v("m"), in_=v("m"), func=AF.Ln, bias=1.0), reads=["gt_m"], writes=["gt_m"])
        yield
        S.op(Dv, "tensor_scalar", dict(out=v("r"), in0=v("x2"), scalar1=0.0, scalar2=None, op0=ALU.max), reads=["gt_x2"], writes=["gt_r"])
        yield
        S.op(Dv, "tensor_tensor", dict(out=v("r"), in0=v("r"), in1=v("m"), op=ALU.add), reads=["gt_r", "gt_m"], writes=["gt_r"])
        yield
        S.op(Dv, "tensor_tensor", dict(out=v("g"), in0=v("r"), in1=self.negA[0:C, :].unsqueeze(1).to_broadcast([C, n, H]), op=ALU.mult),
             reads=["gt_r", "negA"], writes=["gt_g"])
        yield
        g2 = v("g").rearrange("p s h -> p (s h)")
        bk, bkey = self.bank(gbp)
        W = n * H
        S.op(S.pe, "matmul", dict(out=bk[0:C, 0:W], lhsT=self.U[0:C, 0:C], rhs=g2, start=True, stop=True), reads=["U", "gt_g"], writes=[bkey])
        yield
        S.op(S.pe, "matmul", dict(out=bk[0:C, 64:64 + W], lhsT=self.SL8[0:C, 0, 0:C], rhs=g2, start=True, stop=True),
             reads=["SL8", "gt_g"], writes=[bkey])
        yield
        S.op(S.pe, "matmul", dict(out=bk[:, 128:128 + W], lhsT=self.onesf[0:C, :], rhs=g2, start=True, stop=True),
             reads=["onesf", "gt_g"], writes=[bkey])
        yield
        S.op(A, "activation", dict(out=v("egc").rearrange("p s h -> p (s h)"), in_=bk[0:C, 0:W], func=AF.Exp), reads=[bkey], writes=["gt_egc"])
        yield
        S.op(A, "activation", dict(out=v("gcs").rearrange("p s h -> p (s h)"), in_=bk[0:C, 0:W], func=AF.Copy), reads=[bkey], writes=["gt_gcs"])
        yield
        S.op(A, "activation", dict(out=v("ekd").rearrange("p s h -> p (s h)"), in_=bk[0:C, 64:64 + W], func=AF.Exp), reads=[bkey], writes=["gt_ekd"])
        yield
        S.op(A, "activation", dict(out=self.gl[:, s0:s0 + n, :].rearrange("p s h -> p (s h)"), in_=bk[:, 128:128 + W], func=AF.Exp),
             reads=[bkey], writes=["gl"])
        yield
        S.op(Dv, "tensor_tensor", dict(out=v("bge"), in0=v("beta"), in1=v("egc"), op=ALU.mult), reads=["gt_beta", "gt_egc"], writes=["gt_bge"])
        yield

    def delta_prep(self, grp, b0, bp=None):
        S = self.S
        C = grp[0][1]
        ncl = len(grp)
        nb = ncl * H
        s0 = grp[0][2]
        gt = self.gt
        A, Dv, PE = S.act, S.dve, S.pe
        if bp is None:
            bp = {"b": [0, 1], "i": 0}
        bank = lambda: self.bank(bp)

        def gv(name):
            return gt[name][0:C, s0:s0 + ncl, :].rearrange("p s h -> p (s h)")
        f32v = lambda t: t[0:C, 0:nb, 0:C]
        bfv = f32v
        flat = lambda ap: ap.rearrange("p b c -> p (b c)")
        dn3 = lambda t: t[0:C].rearrange("p b c -> p (b c)")[:, 0:nb * C].rearrange("p (b c) -> p b c", c=C)
        v3 = lambda bkp: bkp[0:C, 0:nb * C].rearrange("p (b c) -> p b c", c=C)
        negf = (lambda i: flat(self.NEG[i][:])) if C == CH else (lambda i: self.NEGd[i][:])
        tagc = "m" if C != CH else "r"
        dodump = not hasattr(self, "_dumped_" + tagc)
        setattr(self, "_dumped_" + tagc, True)

        gSL, gSLk = self.take(self.g64f)
        gSLd = dn3(gSL)
        S.op(Dv, "tensor_tensor", dict(out=gSLd, in0=f32v(self.SL8), in1=bc(gv("g"), C), op=ALU.mult), reads=["SL8", "gt_g"], writes=[gSLk])
        dlnb, dlnbk = self.take(self.g64f)
        dlnbd = dn3(dlnb)
        S.op(Dv, "tensor_tensor", dict(out=dlnbd, in0=f32v(self.I8f), in1=bc(gv("lnb"), C), op=ALU.mult), reads=["I8f", "gt_lnb"], writes=[dlnbk])
        bKK, kKK = bank()
        for cl, (col0, _, _) in enumerate(grp):
            for h in range(H):
                b = cl * H + h
                S.op(PE, "matmul", dict(out=bKK[0:C, b * C:(b + 1) * C], lhsT=self.knT[:, h, col0:col0 + C], rhs=self.knT[:, h, col0:col0 + C],
                                        start=True, stop=True), reads=["knT"], writes=[kKK])
        yield
        nKK, nKKk = self.take(self.g64f)
        S.op(A, "activation", dict(out=f32v(nKK), in_=v3(bKK), func=AF.Copy, scale=-1.0), reads=[kKK], writes=[nKKk])
        bA, kA = bank()
        self.mmw(bA[0:C, 0:nb * C], self.U[0:C, 0:C], flat(gSLd), True, False, ["U", gSLk], [kA])
        self.mmw(bA[0:C, 0:nb * C], self.identf[0:C, 0:C], negf(0), False, False, ["identf", "NEG0", "NEGd0"], [kA])
        for b in range(nb):
            S.op(PE, "matmul", dict(out=bA[0:C, b * C:(b + 1) * C], lhsT=dlnbd[:, b, :], rhs=self.onesf[0:C, 0:C], start=False, stop=(b == nb - 1)),
                 reads=[dlnbk, "onesf"], writes=[kA])
        yield
        eA, eAk = self.take(self.g64f)
        S.op(A, "activation", dict(out=f32v(eA), in_=v3(bA), func=AF.Exp), reads=[kA], writes=[eAk])
        if dodump:
            self.dump("eA" + tagc, eA[:], eAk, [CH, 8, CH], F32)
        Y, Yk = self.take(self.g64b)
        S.op(Dv, "tensor_tensor", dict(out=bfv(Y), in0=f32v(nKK), in1=f32v(eA), op=ALU.mult), reads=[nKKk, eAk], writes=[Yk])
        self.rel(self.g64f, eAk)
        bB, kB = bank()
        self.mmw(bB[0:C, 0:nb * C], self.identf[0:C, 0:C], negf(1), True, False, ["identf", "NEG1", "NEGd1"], [kB])
        for b in range(nb):
            S.op(PE, "matmul", dict(out=bB[0:C, b * C:(b + 1) * C], lhsT=gSLd[:, b, :], rhs=self.U[0:C, 0:C], start=False, stop=False),
                 reads=[gSLk, "U"], writes=[kB])
        self.mmw(bB[0:C, 0:nb * C], self.onesf[0:C, 0:C], flat(dlnbd), False, True, ["onesf", dlnbk], [kB])
        yield
        eB, eBk = self.take(self.g64f)
        S.op(A, "activation", dict(out=f32v(eB), in_=v3(bB), func=AF.Exp), reads=[kB], writes=[eBk])
        X, Xk = self.take(self.g64b)
        S.op(Dv, "tensor_tensor", dict(out=bfv(X), in0=f32v(nKK), in1=f32v(eB), op=ALU.mult), reads=[nKKk, eBk], writes=[Xk])
        self.rel(self.g64f, eBk, nKKk)
        Pm, Pk = self.take(self.g64b)
        S.op(Dv, "tensor_tensor", dict(out=bfv(Pm), in0=bfv(X), in1=self.I8b[0:C, 0:nb, 0:C], op=ALU.add), reads=[Xk, "I8b"], writes=[Pk])
        bQ, kQ = bank()
        self.mmw(bQ[0:C, 0:nb * C], self.identf[0:C, 0:C], negf(2), True, False, ["identf", "NEG2", "NEGd2"], [kQ])
        for b in range(nb):
            S.op(PE, "matmul", dict(out=bQ[0:C, b * C:(b + 1) * C], lhsT=gSLd[:, b, :], rhs=self.U[0:C, 0:C], start=False, stop=(b == nb - 1)),
                 reads=[gSLk, "U"], writes=[kQ])
        self.rel(self.g64f, gSLk, dlnbk)
        yield
        eQ, eQk = self.take(self.g64f)
        S.op(A, "activation", dict(out=f32v(eQ), in_=v3(bQ), func=AF.Exp), reads=[kQ], writes=[eQk])
        bKQ, kKQ = bank()
        for cl, (col0, _, _) in enumerate(grp):
            for h in range(H):
                b = cl * H + h
                S.op(PE, "matmul", dict(out=bKQ[0:C, b * C:(b + 1) * C], lhsT=self.knT[:, h, col0:col0 + C], rhs=self.qnT[:, h, col0:col0 + C],
                                        start=True, stop=True), reads=["knT", "qnT"], writes=[kKQ])
        yield
        QKTt, QKTtk = self.take(self.g64b)
        S.op(Dv, "tensor_tensor", dict(out=QKTt[0:C, 0:nb, 0:C], in0=v3(bKQ), in1=f32v(eQ), op=ALU.mult), reads=[kKQ, eQk], writes=[QKTtk])
        self.rel(self.g64f, eQk)
        nlev = 5 if C == CH else 3
        for lev in range(1, nlev + 1):
            last = (lev == nlev)
            bY, kY = bank()
            for b in range(nb):
                S.op(PE, "matmul", dict(out=bY[0:C, b * C:(b + 1) * C], lhsT=X[0:C, b, 0:C], rhs=Y[0:C, b, 0:C], start=True, stop=True),
                     reads=[Xk, Yk], writes=[kY])
            if not last:
                bX, kX = bank()
                for b in range(nb):
                    S.op(PE, "matmul", dict(out=bX[0:C, b * C:(b + 1) * C], lhsT=Y[0:C, b, 0:C], rhs=X[0:C, b, 0:C], start=True, stop=True),
                         reads=[Xk, Yk], writes=[kX])
            self.rel(self.g64b, Xk, Yk)
            yield
            IY, IYk = self.take(self.g64b)
            S.op(Dv, "tensor_tensor", dict(out=bfv(IY), in0=v3(bY), in1=f32v(self.I8f), op=ALU.add), reads=[kY, "I8f"], writes=[IYk])
            if not last:
                Yn, Ynk = self.take(self.g64b)
                Xn, Xnk = self.take(self.g64b)
                S.op(A, "activation", dict(out=bfv(Yn), in_=v3(bY), func=AF.Copy), reads=[kY], writes=[Ynk])
                S.op(A, "activation", dict(out=bfv(Xn), in_=v3(bX), func=AF.Copy), reads=[kX], writes=[Xnk])
            bP, kP = bank()
            for b in range(nb):
                S.op(PE, "matmul", dict(out=bP[0:C, b * C:(b + 1) * C], lhsT=IY[0:C, b, 0:C], rhs=Pm[0:C, b, 0:C], start=True, stop=True),
                     reads=[IYk, Pk], writes=[kP])
            self.rel(self.g64b, IYk, Pk)
            yield
            Pn, Pnk = self.take(self.g64b)
            S.op(A, "activation", dict(out=bfv(Pn), in_=v3(bP), func=AF.Copy), reads=[kP], writes=[Pnk])
            Pm, Pk = Pn, Pnk
            if not last:
                X, Xk, Y, Yk = Xn, Xnk, Yn, Ynk
        TT, TTk = Pm, Pk
        if dodump:
            self.dump("TT" + tagc, TT[:], TTk, [CH, 8, CH], BF16)
        dgc, dgck = self.take(self.g64f)
        dgcd = dn3(dgc)
        S.op(Dv, "tensor_tensor", dict(out=dgcd, in0=f32v(self.I8f), in1=bc(gv("gcs"), C), op=ALU.mult), reads=["I8f", "gt_gcs"], writes=[dgck])
        bE, kE = bank()
        self.mmw(bE[:, 0:nb * C], self.onesf[0:C, :], flat(dgcd), True, True, ["onesf", dgck], [kE])
        self.rel(self.g64f, dgck)
        yield
        Eg, Egk = self.take(self.colpool)
        S.op(A, "activation", dict(out=Eg[:, 0:nb * C], in_=bE[:, 0:nb * C], func=AF.Exp), reads=[kE], writes=[Egk])
        for cl, (colc, _, _) in enumerate(grp):
            Ev = Eg[:, cl * H * C:(cl + 1) * H * C].rearrange("p (h i) -> p h i", h=H)
            S.op(Dv, "tensor_tensor", dict(out=self.qdT[:, :, colc:colc + C], in0=self.qnT[:, :, colc:colc + C], in1=Ev, op=ALU.mult),
                 reads=["qnT", Egk], writes=["qdT"])
        yield "SYNC"
        S.op(S.pool, "tensor_copy", dict(out=self.QKT[0:C, b0:b0 + nb, 0:C], in_=QKTt[0:C, 0:nb, 0:C]), reads=[QKTtk], writes=["QKT"])
        self.rel(self.g64b, QKTtk)
        bK, kK = bank()
        bV, kV = bank()
        bKt, bVt = bK.bitcast(BF16), bV.bitcast(BF16)
        for cl, (col0, _, _) in enumerate(grp):
            for h in range(H):
                b = cl * H + h
                S.op(PE, "transpose", dict(out=bKt[0:C, b * DH:(b + 1) * DH], in_=self.knT[:, h, col0:col0 + C], identity=self.identb[:]),
                     reads=["knT", "identb"], writes=[kK])
        for cl, (col0, _, _) in enumerate(grp):
            for h in range(H):
                b = cl * H + h
                S.op(PE, "transpose", dict(out=bVt[0:C, b * DH:(b + 1) * DH], in_=self.vT[:, h, col0:col0 + C], identity=self.identb[:]),
                     reads=["vT", "identb"], writes=[kV])
        yield
        k3 = bKt[0:C, 0:nb * DH].rearrange("p (b d) -> p b d", d=DH)
        v3_ = bVt[0:C, 0:nb * DH].rearrange("p (b d) -> p b d", d=DH)
        kbg, kbgk = self.take(self.kbgr)
        S.op(Dv, "tensor_tensor", dict(out=kbg[0:C, 0:nb, :], in0=k3, in1=bc(gv("bge"), DH), op=ALU.mult), reads=[kK, "gt_bge"], writes=[kbgk])
        S.op(Dv, "tensor_tensor", dict(out=self.kdec[0:C, b0:b0 + nb, :], in0=k3, in1=bc(gv("ekd"), DH), op=ALU.mult), reads=[kK, "gt_ekd"], writes=["kdec"])
        S.op(Dv, "tensor_tensor", dict(out=self.vb[0:C, b0:b0 + nb, :], in0=v3_, in1=bc(gv("beta"), DH), op=ALU.mult), reads=[kV, "gt_beta"], writes=["vb"])
        bW, kW = bank()
        for b in range(nb):
            S.op(PE, "matmul", dict(out=bW[:, b * C:(b + 1) * C], lhsT=kbg[0:C, b, :], rhs=TT[0:C, b, 0:C], start=True, stop=True),
                 reads=[kbgk, TTk], writes=[kW])
        yield
        S.op(A, "activation", dict(out=self.wT[:, b0:b0 + nb, 0:C], in_=bW[:, 0:nb * C].rearrange("p (b c) -> p b c", c=C), func=AF.Copy),
             reads=[kW], writes=["wT"])
        for q in range(0, nb, 4):
            bU, kU = bank()
            for b in range(q, min(q + 4, nb)):
                S.op(PE, "matmul", dict(out=bU[0:C, (b - q) * DH:(b - q + 1) * DH], lhsT=TT[0:C, b, 0:C], rhs=self.vb[0:C, b0 + b, :],
                                        start=True, stop=True), reads=[TTk, "vb"], writes=[kU])
            yield
            nq = min(4, nb - q)
            S.op(A, "activation", dict(out=self.u[0:C, b0 + q:b0 + q + nq, :], in_=bU[0:C, 0:nq * DH].rearrange("p (b d) -> p b d", d=DH), func=AF.Copy),
                 reads=[kU], writes=["u"])
        self.rel(self.g64b, TTk)
        if dodump:
            self.dump("u" + tagc, self.u, "u", [CH, self.HB, DH], F32)
            self.dump("wT" + tagc, self.wT, "wT", [P, self.HB, CH], BF16)
            self.dump("QKT" + tagc, self.QKT, "QKT", [CH, self.HB, CH], BF16)
            self.dump("kdec" + tagc, self.kdec, "kdec", [CH, self.HB, DH], BF16)
            self.dump("vb" + tagc, self.vb, "vb", [CH, self.HB, DH], BF16)

    def chain_chunk(self, col0, C, slot, bb, oinfo):
        S = self.S
        A, Dv, PE = S.act, S.dve, S.pe
        bWS, kWS = self.bank(self.chain_pool)
        for h in range(H):
            S.op(PE, "matmul", dict(out=bWS[0:C, h * DH:(h + 1) * DH], lhsT=self.wT[:, bb + h, 0:C], rhs=self.Sbf[:, h, :], start=True, stop=True),
                 reads=["wT", "Sbf"], writes=[kWS])
        S.op(Dv, "tensor_tensor", dict(out=self.vnew[0:C, :, :], in0=self.u[0:C, bb:bb + H, :],
                                       in1=bWS[0:C, :].rearrange("p (h d) -> p h d", d=DH), op=ALU.subtract),
             reads=["u", kWS], writes=["vnew"])
        yield
        if oinfo is not None:
            ob, okey, i = oinfo
            for h in range(H):
                reg = ob[:, (i * H + h) * CH:(i * H + h + 1) * CH]
                S.op(PE, "matmul", dict(out=reg, lhsT=self.Sbf[:, h, :], rhs=self.qdT[:, h, col0:col0 + C], start=True, stop=False),
                     reads=["Sbf", "qdT"], writes=[okey])
                S.op(PE, "matmul", dict(out=reg, lhsT=self.vnew[0:C, h, :], rhs=self.QKT[0:C, bb + h, 0:C], start=False, stop=True),
                     reads=["vnew", "QKT"], writes=[okey])
        yield
        bSd, kSd = self.bank(self.chain_pool)
        for h in range(H):
            S.op(PE, "matmul", dict(out=bSd[:, h * DH:(h + 1) * DH], lhsT=self.kdec[0:C, bb + h, :], rhs=self.vnew[0:C, h, :], start=True, stop=True),
                 reads=["kdec", "vnew"], writes=[kSd])
        yield
        for h in range(H):
            S.op(Dv, "scalar_tensor_tensor", dict(out=self.Sst[:, h, :], in0=self.Sst[:, h, :], scalar=self.gl[:, slot, h:h + 1],
                                                  in1=bSd[:, h * DH:(h + 1) * DH], op0=ALU.mult, op1=ALU.add),
                 reads=["Sst", "gl", kSd], writes=["Sst"])
        yield
        S.op(A, "activation", dict(out=self.Sbf[:], in_=self.Sst[:], func=AF.Copy), reads=["Sst"], writes=["Sbf"])
        yield

    def o_norm(self, ob, okey, col0):
        S = self.S
        A, Dv, PE = S.act, S.dve, S.pe
        osb, osbk = self.take(self.t128)
        S.op(A, "activation", dict(out=osb[:], in_=ob[:, :], func=AF.Copy), reads=[okey], writes=[osbk])
        S.op(A, "activation", dict(out=self.osq[:], in_=ob[:, :], func=AF.Square), reads=[okey], writes=["osq"])
        yield
        bN, kN = self.bank(self.chain_pool)
        S.op(PE, "matmul", dict(out=bN[:, :], lhsT=self.onesb[:], rhs=self.osq[:], start=True, stop=True), reads=["onesb", "osq"], writes=[kN])
        yield
        rs, rsk = self.take(self.t128)
        S.op(A, "activation", dict(out=rs[:], in_=bN[:, :], func=AF.Ln, scale=1.0 / DH, bias=EPS), reads=[kN], writes=[rsk])
        S.op(A, "activation", dict(out=rs[:], in_=rs[:], func=AF.Exp, scale=-0.5), reads=[rsk], writes=[rsk])
        yield
        S.op(Dv, "tensor_tensor", dict(out=osb[:], in0=osb[:], in1=rs[:], op=ALU.mult), reads=[osbk, rsk], writes=[osbk])
        for c in range(2):
            ov = osb[:, c * H * CH:(c + 1) * H * CH].rearrange("p (h i) -> p h i", h=H)
            cc0 = col0 + c * CH
            S.op(Dv, "scalar_tensor_tensor", dict(out=self.yT[:, 0:H, cc0:cc0 + CH], in0=ov, scalar=self.gdnw[:, 0:1],
                                                  in1=self.zsT[:, :, cc0:cc0 + CH], op0=ALU.mult, op1=ALU.mult),
                 reads=[osbk, "gdnw", "zsT"], writes=["yT"])


_CACHE = {}


def _get_builder(cfg_key, dbg=()):
    key = (cfg_key, tuple(dbg))
    if key not in _CACHE:
        _CACHE[key] = Builder(Cfg(*cfg_key), dbg)
    return _CACHE[key]


def make_in_maps(cfg, n_cores, x, meta_tokens, mix_pre_norm, mix_post_norm, ffn_pre_norm, ffn_post_norm, w_in, conv_qkv,
                 a_log, dt_bias, gdn_norm, conv_sc, w_out, w_gate, w_up, w_down):
    f = lambda a: np.ascontiguousarray(np.asarray(a, dtype=np.float32))
    shared = {
        "meta": f(meta_tokens), "n_pre": f(mix_pre_norm).reshape(D), "n_post": f(mix_post_norm).reshape(D),
        "n_fpre": f(ffn_pre_norm).reshape(D), "n_fpost": f(ffn_post_norm).reshape(D),
        "w_in": f(w_in).reshape(D, INW), "cqkv": f(conv_qkv).reshape(4, 1536), "a_log": f(a_log).reshape(H),
        "dt_bias": f(dt_bias).reshape(H), "gdn": f(gdn_norm).reshape(DH), "csc": f(conv_sc).reshape(3, 512),
        "w_out": f(w_out).reshape(D, D), "w_gate": f(w_gate).reshape(D, DFF), "w_up": f(w_up).reshape(D, DFF),
        "w_down": f(w_down).reshape(DFF, D),
    }
    x = f(x)
    maps = []
    for c in range(n_cores):
        m = dict(shared)
        m["x"] = np.ascontiguousarray(x[c * cfg.nseq:(c + 1) * cfg.nseq].reshape(cfg.nseq * cfg.seq, D))
        maps.append(m)
    return maps


def kernel(x, meta_tokens, mix_pre_norm, mix_post_norm, ffn_pre_norm, ffn_post_norm, w_in, conv_qkv,
           a_log, dt_bias, gdn_norm, conv_sc, w_out, w_gate, w_up, w_down):
    x = np.asarray(x)
    B, SEQ, _ = x.shape
    nseq = B // N_CORES
    b = _get_builder((nseq, SEQ, 512))
    maps = make_in_maps(b.cfg, N_CORES, x, meta_tokens, mix_pre_norm, mix_post_norm, ffn_pre_norm, ffn_post_norm, w_in,
                        conv_qkv, a_log, dt_bias, gdn_norm, conv_sc, w_out, w_gate, w_up, w_down)
    res = run_bass_kernel_spmd(b.nc, maps, core_ids=list(range(N_CORES)))
    outs = [np.asarray(r["out"]).reshape(nseq, SEQ, D) for r in res.results]
    return np.concatenate(outs, axis=0).astype(np.float32)
```
